# Optimizing a Trainium2 kernel written in Bass

```python
import jax, jax.numpy as jnp
from jax import lax
import numpy as np

D_MODEL = 1024
BATCH = 8
SEQ = 4096
DEPTH = 4

N_MIXERS = 2
N_ATTN_LAYERS = (DEPTH + 1) // 2
N_SGU_LAYERS = DEPTH // 2

N_HEADS = 16
N_KV_HEADS = 4
HEAD_DIM = 64
Q_PER_KV = N_HEADS // N_KV_HEADS
WINDOW = 128
BLOCK = 128
Q_DIM = N_HEADS * HEAD_DIM
KV_DIM = N_KV_HEADS * HEAD_DIM
QKV_DIM = Q_DIM + 2 * KV_DIM
ALIBI_MAX_BIAS = 8.0

CHUNK = 128
SGU_HALF = 3 * D_MODEL
N_SGU_GROUPS = 8
SGU_GROUP_DIM = SGU_HALF // N_SGU_GROUPS

D_FF = 2816
CONV_WIDTH = 3

EPS = 1e-6

kernel_name = "hybrid_swa_sgu_convffn_trunk"


def rms_norm(x, g):
    xf = x.astype(jnp.float32)
    y = xf * lax.rsqrt(jnp.mean(xf * xf, axis=-1, keepdims=True) + EPS)
    return (y * g.astype(jnp.float32)).astype(x.dtype)


def layer_norm(x, g, b):
    xf = x.astype(jnp.float32)
    mu = jnp.mean(xf, axis=-1, keepdims=True)
    xc = xf - mu
    y = xc * lax.rsqrt(jnp.mean(xc * xc, axis=-1, keepdims=True) + EPS)
    return (y * g.astype(jnp.float32) + b.astype(jnp.float32)).astype(x.dtype)


def alibi_slopes():
    h = jnp.arange(1, N_HEADS + 1, dtype=jnp.float32)
    return jnp.exp2(-ALIBI_MAX_BIAS * h / N_HEADS)


def sliding_window_attention(h, w_qkv, b_qkv, sinks, w_o, b_o):
    B, S, _ = h.shape
    nb = S // BLOCK
    qkv = h @ w_qkv + b_qkv
    q, k, v = jnp.split(qkv, [Q_DIM, Q_DIM + KV_DIM], axis=-1)
    q = q.reshape(B, nb, BLOCK, N_KV_HEADS, Q_PER_KV, HEAD_DIM)
    k = k.reshape(B, nb, BLOCK, N_KV_HEADS, HEAD_DIM)
    v = v.reshape(B, nb, BLOCK, N_KV_HEADS, HEAD_DIM)

    def with_prev(t):
        prev = jnp.concatenate([jnp.zeros_like(t[:, :1]), t[:, :-1]], axis=1)
        return jnp.concatenate([prev, t], axis=2)

    kb, vb = with_prev(k), with_prev(v)
    scores = jnp.einsum("bnqgrd,bnkgd->bngrqk", q, kb).astype(jnp.float32) * (HEAD_DIM ** -0.5)

    qi = jnp.arange(BLOCK)[:, None]
    kj = jnp.arange(2 * BLOCK)[None, :]
    dist = qi + BLOCK - kj
    key_pos = jnp.arange(nb)[:, None, None] * BLOCK - BLOCK + kj[None]
    valid = (dist >= 0)[None] & (dist < WINDOW)[None] & (key_pos >= 0)

    slopes = alibi_slopes().reshape(N_KV_HEADS, Q_PER_KV)
    scores = scores - slopes[:, :, None, None] * dist.astype(jnp.float32)
    scores = jnp.where(valid[None, :, None, None], scores, -jnp.inf)

    sink = sinks.astype(jnp.float32).reshape(N_KV_HEADS, Q_PER_KV)[None, None, :, :, None, None]
    m = jnp.maximum(jnp.max(scores, axis=-1, keepdims=True), sink)
    p = jnp.exp(scores - m)
    probs = p / (jnp.sum(p, axis=-1, keepdims=True) + jnp.exp(sink - m))

    out = jnp.einsum("bngrqk,bnkgd->bnqgrd", probs.astype(h.dtype), vb)
    return out.reshape(B, S, Q_DIM) @ w_o + b_o


def chunked_sgu(h, w_in, ln_g, ln_b, w_s, b_s, w_out):
    B, S, _ = h.shape
    nc = S // CHUNK
    z = jax.nn.gelu(h @ w_in)
    u, v = jnp.split(z, 2, axis=-1)
    v = layer_norm(v, ln_g, ln_b)
    v = v.reshape(B, nc, CHUNK, N_SGU_GROUPS, SGU_GROUP_DIM)
    causal = jnp.tril(jnp.ones((CHUNK, CHUNK), dtype=bool))
    ws = jnp.where(causal[None], w_s, jnp.zeros_like(w_s))
    sv = jnp.einsum("gts,bcsgd->bctgd", ws, v) + b_s.T[None, None, :, :, None]
    return (u * sv.reshape(B, S, SGU_HALF)) @ w_out


def conv_gated_ffn(h, w_in, conv_w, conv_b, w_out):
    S = h.shape[1]
    g, u = jnp.split(h @ w_in, 2, axis=-1)
    gp = jnp.pad(g, ((0, 0), (CONV_WIDTH - 1, 0), (0, 0)))
    g = conv_b + sum(conv_w[k] * gp[:, k:k + S] for k in range(CONV_WIDTH))
    return (jax.nn.gelu(g) * u) @ w_out


def setup_inputs(seed: int = 0) -> dict:
    key = jax.random.key(seed)
    ks = jax.random.split(key, 24)
    f32 = jnp.float32

    def nrm(k, shape, scale):
        return jax.random.normal(k, shape, f32) * scale

    nA, nB = N_ATTN_LAYERS, N_SGU_LAYERS
    return {
        "x": nrm(ks[0], (BATCH, SEQ, D_MODEL), 1.0),
        "attn_w_qkv": nrm(ks[1], (nA, D_MODEL, QKV_DIM), D_MODEL ** -0.5),
        "attn_b_qkv": nrm(ks[2], (nA, QKV_DIM), 0.02),
        "attn_sinks": nrm(ks[3], (nA, N_HEADS), 1.0),
        "attn_w_o": nrm(ks[4], (nA, Q_DIM, D_MODEL), Q_DIM ** -0.5),
        "attn_b_o": nrm(ks[5], (nA, D_MODEL), 0.02),
        "sgu_w_in": nrm(ks[6], (nB, D_MODEL, 2 * SGU_HALF), D_MODEL ** -0.5),
        "sgu_ln_g": 1.0 + nrm(ks[7], (nB, SGU_HALF), 0.05),
        "sgu_ln_b": nrm(ks[8], (nB, SGU_HALF), 0.02),
        "sgu_w_s": nrm(ks[9], (nB, N_SGU_GROUPS, CHUNK, CHUNK), CHUNK ** -0.5),
        "sgu_b_s": 1.0 + nrm(ks[10], (nB, N_SGU_GROUPS, CHUNK), 0.05),
        "sgu_w_out": nrm(ks[11], (nB, SGU_HALF, D_MODEL), SGU_HALF ** -0.5),
        "ffn_w_in": nrm(ks[12], (DEPTH, D_MODEL, 2 * D_FF), D_MODEL ** -0.5),
        "ffn_conv_w": nrm(ks[13], (DEPTH, CONV_WIDTH, D_FF), CONV_WIDTH ** -0.5),
        "ffn_conv_b": nrm(ks[14], (DEPTH, D_FF), 0.02),
        "ffn_w_out": nrm(ks[15], (DEPTH, D_FF, D_MODEL), D_FF ** -0.5),
        "norm_mix_pre": 1.0 + nrm(ks[16], (DEPTH, D_MODEL), 0.05),
        "norm_mix_post": 1.0 + nrm(ks[17], (DEPTH, D_MODEL), 0.05),
        "norm_ffn_pre": 1.0 + nrm(ks[18], (DEPTH, D_MODEL), 0.05),
        "norm_ffn_post": 1.0 + nrm(ks[19], (DEPTH, D_MODEL), 0.05),
    }


def reference(x, attn_w_qkv, attn_b_qkv, attn_sinks, attn_w_o, attn_b_o,
              sgu_w_in, sgu_ln_g, sgu_ln_b, sgu_w_s, sgu_b_s, sgu_w_out,
              ffn_w_in, ffn_conv_w, ffn_conv_b, ffn_w_out,
              norm_mix_pre, norm_mix_post, norm_ffn_pre, norm_ffn_post):
    for i in range(DEPTH):
        j = i // N_MIXERS
        h = rms_norm(x, norm_mix_pre[i])
        if i % N_MIXERS == 0:
            h = sliding_window_attention(h, attn_w_qkv[j], attn_b_qkv[j], attn_sinks[j],
                                         attn_w_o[j], attn_b_o[j])
        else:
            h = chunked_sgu(h, sgu_w_in[j], sgu_ln_g[j], sgu_ln_b[j], sgu_w_s[j],
                            sgu_b_s[j], sgu_w_out[j])
        x = x + rms_norm(h, norm_mix_post[i])
        h = conv_gated_ffn(rms_norm(x, norm_ffn_pre[i]), ffn_w_in[i], ffn_conv_w[i],
                           ffn_conv_b[i], ffn_w_out[i])
        x = x + rms_norm(h, norm_ffn_post[i])
    return x
```

```python
import numpy as np
from contextlib import ExitStack
import concourse.bass as bass
import concourse.mybir as mybir
from concourse.bass_utils import run_bass_kernel_spmd

F32 = mybir.dt.float32
BF16 = mybir.dt.bfloat16
AF = mybir.ActivationFunctionType
ALU = mybir.AluOpType

D = 1024
TT = 512
DFF = 2816
SH = 3072
EPS = 1e-6
GA = float(np.sqrt(0.044715))
GC = float(np.sqrt(2.0 / np.pi))
CLAMP = 480.0
NEG = -1.0e5
NSLOT = 3
SLOT_E = 8192


class Sync:
    def __init__(self, nc, ctx):
        self.nc = nc
        self.eng = {"pe": nc.tensor, "act": nc.scalar, "dve": nc.vector, "pool": nc.gpsimd, "sp": nc.sync}
        self.sems, self.count, self.ctx = {}, {}, ctx
        for k in self.eng:
            self.sems[k] = ctx.enter_context(nc.semaphore("sem_" + k))
            self.count[k] = 0
        self.waited = {k: {} for k in self.eng}
        self.last_w, self.readers = {}, {}

    def new_sem(self, name):
        key = "d_" + name
        self.sems[key] = self.ctx.enter_context(self.nc.semaphore(key))
        self.count[key] = 0
        return key

    def _wait(self, e, dep):
        sk, v = dep
        if sk == "pe" and e == "pe":
            return
        if self.waited[e].get(sk, 0) >= v:
            return
        self.eng[e].wait_ge(self.sems[sk], v)
        self.waited[e][sk] = v

    def _deps(self, e, reads, writes):
        for r in reads:
            if r in self.last_w:
                self._wait(e, self.last_w[r])
        for w in writes:
            if w in self.last_w:
                self._wait(e, self.last_w[w])
            for d in self.readers.get(w, ()):
                self._wait(e, d)

    def _commit(self, tag, reads, writes):
        for w in writes:
            self.last_w[w] = tag
            self.readers[w] = []
        for r in reads:
            self.readers.setdefault(r, []).append(tag)

    def op(self, e, ins_fn, reads=(), writes=(), inc=True):
        self._deps(e, reads, writes)
        ins = ins_fn()
        tag = (e, self.count[e] + 1)
        if inc:
            self.count[e] += 1
            ins.then_inc(self.sems[e], 1)
        self._commit(tag, reads, writes)
        return ins

    def dma(self, q, semkey, out, in_, reads=(), writes=()):
        self._deps(q, reads, writes)
        ins = self.eng[q].dma_start(out=out, in_=in_)
        self.count[semkey] += 16
        ins.then_inc(self.sems[semkey], 16)
        self._commit((semkey, self.count[semkey]), reads, writes)
        return ins

    def fence(self, prefix):
        deps = set()
        for k in list(self.last_w):
            if k.startswith(prefix):
                deps.add(self.last_w.pop(k))
        for k in list(self.readers):
            if k.startswith(prefix):
                deps.update(self.readers.pop(k))
        for e in ("pe", "act", "dve", "pool"):
            for d in deps:
                self._wait(e, d)

    def wait_final(self, e, keys):
        for b in keys:
            if b in self.last_w:
                self._wait(e, self.last_w[b])


def _colvec(v):
    v = np.asarray(v, np.float32)
    return np.ascontiguousarray(v.reshape(-1, 128).T)


def pack_params(inp):
    cols, off, cur = [], {}, 0

    def add(name, arr):
        nonlocal cur
        arr = np.asarray(arr, np.float32).reshape(128, -1)
        off[name] = cur
        cols.append(arr)
        cur += arr.shape[1]

    for L in range(4):
        add(f"nmp{L}", _colvec(inp["norm_mix_pre"][L]))
        add(f"nmo{L}", _colvec(inp["norm_mix_post"][L]))
        add(f"nfp{L}", _colvec(inp["norm_ffn_pre"][L]))
        add(f"nfo{L}", _colvec(inp["norm_ffn_post"][L]))
        cw = np.asarray(inp["ffn_conv_w"][L], np.float32)
        add(f"cw{L}", np.ascontiguousarray(cw.reshape(3, 22, 128).transpose(2, 1, 0)))
        add(f"cb{L}", _colvec(inp["ffn_conv_b"][L]))
    for j in range(2):
        bqkv = np.asarray(inp["attn_b_qkv"][j], np.float32)
        add(f"bq{j}", _colvec(bqkv[:1024]))
        bk = bqkv[1024:1280].reshape(4, 64)
        add(f"bk{j}", np.ascontiguousarray(np.concatenate([bk, bk], axis=1).T))
        add(f"bo{j}", _colvec(inp["attn_b_o"][j]))
        add(f"bv{j}", np.broadcast_to(bqkv[1280:1536][None, :], (128, 256)))
        sk = np.asarray(inp["attn_sinks"][j], np.float32)
        perm = [h for g in range(4) for h in (4 * g, 4 * g + 2, 4 * g + 1, 4 * g + 3)]
        add(f"sk{j}", np.broadcast_to(sk[perm][None, :], (128, 16)))
        add(f"lg{j}", _colvec(inp["sgu_ln_g"][j]))
        add(f"lb{j}", _colvec(inp["sgu_ln_b"][j]))
        add(f"bs{j}", np.broadcast_to(np.asarray(inp["sgu_b_s"][j], np.float32).reshape(1, 1024), (128, 1024)))
    add("ident", np.eye(128, dtype=np.float32))
    prm = np.ascontiguousarray(np.concatenate(cols, axis=1))
    ws = np.asarray(inp["sgu_w_s"], np.float32)
    wsT = np.ascontiguousarray(ws.transpose(0, 3, 1, 2).reshape(2, 128, 1024))
    s_i = np.arange(128)[:, None]
    t_i = np.arange(128)[None, :]
    mask = (s_i <= t_i).astype(np.float32)
    maskT = np.ascontiguousarray(np.tile(mask, (1, 8)))
    slopes = np.exp2(-8.0 * np.arange(1, 17, dtype=np.float64) / 16.0)
    k_i = np.arange(128)[:, None]
    q_i = np.arange(128)[None, :]
    bias = np.zeros((128, 2, 4, 2, 2, 128), np.float32)
    for xy in range(2):
        for g in range(4):
            for hh in range(2):
                h = 4 * g + 2 * hh + xy
                dist_c = (q_i - k_i).astype(np.float64)
                cur_b = np.where(k_i <= q_i, -slopes[h] * dist_c * 8.0, NEG)
                dist_p = (q_i + 128 - k_i).astype(np.float64)
                prv_b = np.where(k_i > q_i, -slopes[h] * dist_p * 8.0, NEG)
                bias[:, xy, g, 0, hh, :] = prv_b
                bias[:, xy, g, 1, hh, :] = cur_b
    bias = np.ascontiguousarray(bias.reshape(128, 4096))
    return prm, off, wsT, maskT, bias


class Builder:
    def __init__(self, S, off, nprm, nlayers=4):
        self.S, self.off, self.nprm, self.nlayers = S, off, nprm, nlayers
        self.NT = S // TT

    def bank(self):
        self.pb = (self.pb + 1) % self.nbank
        return self.pb

    def P(self, name, c=0, n=1):
        o = self.off[name] + c
        return self.prm[:, o:o + n]

    def mm(self, b, cols, lhsT, rhs, start, stop, reads, last):
        nc = self.nc
        out = self.ps[:, b, cols[0]:cols[1]]
        self.S_.op("pe", lambda: nc.tensor.matmul(out, lhsT=lhsT, rhs=rhs, start=start, stop=stop),
                   reads=reads, writes=[f"ps{b}"], inc=last)

    def build(self):
        nc = bass.Bass("TRN2", target_bir_lowering=False)
        self.nc = nc
        S = self.S
        dt = nc.dram_tensor
        self.x_d = dt("x", [S, D], F32, kind="ExternalInput").ap()
        self.prm_d = dt("prm", [128, self.nprm], F32, kind="ExternalInput").ap()
        self.wsT_d = dt("wsT", [2, 128, 1024], F32, kind="ExternalInput").ap()
        self.mask_d = dt("maskT", [128, 1024], F32, kind="ExternalInput").ap()
        self.bias_d = dt("abias", [128, 4096], F32, kind="ExternalInput").ap()
        self.wqkv_d = dt("attn_w_qkv", [2, D, 1536], F32, kind="ExternalInput").ap()
        self.wo_d = dt("attn_w_o", [2, D, D], F32, kind="ExternalInput").ap()
        self.swin_d = dt("sgu_w_in", [2, D, 2 * SH], F32, kind="ExternalInput").ap()
        self.swout_d = dt("sgu_w_out", [2, SH, D], F32, kind="ExternalInput").ap()
        self.fwin_d = dt("ffn_w_in", [4, D, 2 * DFF], F32, kind="ExternalInput").ap()
        self.fwout_d = dt("ffn_w_out", [4, DFF, D], F32, kind="ExternalInput").ap()
        self.y_d = dt("y", [S, D], F32, kind="ExternalOutput").ap()
        self.s_wq = [dt(f"s_wq{j}", [D, 1024], BF16, kind="Internal").ap() for j in range(2)]
        self.s_wkd = [dt(f"s_wkd{j}", [D, 512], BF16, kind="Internal").ap() for j in range(2)]
        self.s_wv = [dt(f"s_wv{j}", [D, 256], BF16, kind="Internal").ap() for j in range(2)]
        self.s_wo = [dt(f"s_wo{j}", [D, D], BF16, kind="Internal").ap() for j in range(2)]
        self.s_swin = [dt(f"s_swin{j}", [D, 2 * SH], BF16, kind="Internal").ap() for j in range(2)]
        self.s_swout = [dt(f"s_swout{j}", [SH, D], BF16, kind="Internal").ap() for j in range(2)]
        self.s_fwin = [dt(f"s_fwin{L}", [D, 2 * DFF], BF16, kind="Internal").ap() for L in range(4)]
        self.s_fwout = [dt(f"s_fwout{L}", [DFF, D], BF16, kind="Internal").ap() for L in range(4)]

        with ExitStack() as ctx:
            self.S_ = Sync(nc, ctx)
            sb = lambda name, shape, dtp: ctx.enter_context(nc.sbuf_tensor(name, shape, dtp))
            self.prm = sb("prm_sb", [128, self.nprm], F32)
            self.abias = sb("abias_sb", [128, 2, 4096], BF16)
            self.identB = sb("identB", [128, 128], BF16)
            self.xT = sb("xT", [128, 8, TT], F32)
            self.hT = sb("hT", [128, 8, TT], BF16)
            self.gated = sb("gated", [128, 24, TT], BF16)
            self.ybuf = sb("ybuf", [128, 8, TT], F32)
            self.wring = sb("wring", [128, NSLOT, SLOT_E], BF16)
            self.onesD = sb("onesD", [128, 128], BF16)
            self.ones1 = sb("ones1", [128, 128], BF16)
            self.cneg = sb("cneg", [128, 8], F32)
            self.epsc = sb("epsc", [128, 1], F32)
            self.nt1 = sb("nt1", [128, TT], F32)
            self.rstd = sb("rstd", [128, TT], F32)
            self.halo = sb("halo", [128, 4, 22, 2], F32)
            self.wsTm = sb("wsTm", [128, 2, 8, 128], BF16)
            self.kcarry = sb("kcarry", [128, 2, 2, 4, 128], BF16)
            self.vcarry = sb("vcarry", [128, 2, 512], BF16)
            self.small = sb("small", [128, 64], F32)
            self.ARENA = 9400
            self.arena = sb("arena", [128, self.ARENA], F32)
            self.ps = ctx.enter_context(nc.psum_tensor("ps", [128, 8, TT], F32))
            self.pb = 0
            self.nbank = 7
            self.prologue()
            self.make_wplan()
            self.wnext_issue = 0
            self.wnext_use = 0
            for t in range(self.NT):
                self.load_x(t)
                for L in range(self.nlayers):
                    if L % 2 == 0:
                        self.attention(L, L // 2, t)
                    else:
                        self.sgu(L, L // 2, t)
                    self.ffn(L, t)
                self.store_x(t)
            self.S_.wait_final("sp", ["ydram"])
        return nc

    def carve(self, specs):
        out, o = {}, 0
        for name, n, dtp in specs:
            ncol = n if dtp == F32 else (n + 1) // 2
            ap = self.arena[:, o:o + ncol]
            if dtp != F32:
                ap = ap.bitcast(dtp)
            out[name] = ap
            o += ncol
        assert o <= self.ARENA, (o, self.ARENA)
        return out

    def prologue(self):
        nc, S_ = self.nc, self.S_
        d = S_.new_sem("prm")
        S_.dma("sp", d, self.prm[:], self.prm_d[:, :], writes=["prm"])
        a = self.carve([("ab", 4096, F32), ("ws", 2048, F32), ("mk", 1024, F32)])
        ab32 = a["ab"]
        d = S_.new_sem("abias")
        S_.dma("sp", d, ab32, self.bias_d[:, :], writes=["ar_ab"])
        S_.op("dve", lambda: nc.vector.tensor_copy(out=self.abias[:, 0, :], in_=ab32), reads=["ar_ab"], writes=["abias"])
        S_.op("dve", lambda: nc.vector.tensor_tensor(out=self.abias[:, 1, :], in0=ab32, in1=self.abias[:, 0, :], op=ALU.subtract),
              reads=["ar_ab", "abias"], writes=["abias"])
        S_.op("dve", lambda: nc.vector.tensor_copy(out=self.identB[:], in_=self.P("ident", 0, 128)), reads=["prm"], writes=["identB"])
        S_.fence("ar_")
        S_.op("dve", lambda: nc.vector.memset(self.onesD[:], 1.0 / D), writes=["onesD"])
        S_.op("dve", lambda: nc.vector.memset(self.ones1[:], 1.0), writes=["ones1"])
        S_.op("dve", lambda: nc.vector.memset(self.cneg[:], -0.5), writes=["cneg"])
        S_.op("dve", lambda: nc.vector.memset(self.epsc[:], EPS), writes=["epsc"])
        S_.op("dve", lambda: nc.vector.memset(self.halo[:].rearrange("p a b c -> p (a b c)"), 0.0), writes=[f"halo{c}" for c in range(22)])
        S_.op("dve", lambda: nc.vector.memset(self.kcarry[:].rearrange("p a e b c -> p (a e b c)"), 0.0), writes=["kcarry"])
        S_.op("dve", lambda: nc.vector.memset(self.vcarry[:].rearrange("p a b -> p (a b)"), 0.0), writes=["vcarry"])
        d = S_.new_sem("ws")
        for j in range(2):
            S_.dma("sp", d, a["ws"][:, j * 1024:(j + 1) * 1024], self.wsT_d[j], writes=["ar_ws"])
        S_.dma("sp", d, a["mk"], self.mask_d[:, :], writes=["ar_mk"])
        for j in range(2):
            S_.op("dve", lambda j=j: nc.vector.tensor_tensor(
                out=self.wsTm[:, j].rearrange("p g t -> p (g t)"), in0=a["ws"][:, j * 1024:(j + 1) * 1024],
                in1=a["mk"], op=ALU.mult), reads=["ar_ws", "ar_mk"], writes=["wsTm"])
        S_.fence("ar_")
        self.wkey = {}

    def make_wplan(self):
        plan = []

        def blk(KC, segs):
            assert sum(KC * s[2] for s in segs) <= SLOT_E
            plan.append((KC, segs))

        for L in range(self.nlayers):
            j = L // 2
            if L % 2 == 0:
                blk(8, [(self.s_wq[j], 0, 1024, f"wq{j}", self.wqkv_d[j], 0, False)])
                blk(8, [(self.s_wkd[j], 0, 512, f"wkd{j}", self.wqkv_d[j], 1024, True),
                        (self.s_wv[j], 0, 256, f"wv{j}", self.wqkv_d[j], 1280, False)])
                blk(8, [(self.s_wo[j], 0, 1024, f"wo{j}", self.wo_d[j], 0, False)])
            else:
                for i in range(3):
                    blk(8, [(self.s_swin[j], SH + i * 1024, 1024, f"swin{j}", self.swin_d[j], 0, False)])
                for i in range(3):
                    blk(8, [(self.s_swin[j], i * 1024, 1024, f"swin{j}", self.swin_d[j], 0, False)])
                for i in range(4):
                    blk(24, [(self.s_swout[j], i * 256, 256, f"swout{j}", self.swout_d[j], 0, False)])
            for i in range(6):
                n = 512 if i < 5 else 256
                blk(8, [(self.s_fwin[L], i * 512, n, f"fwin{L}", self.fwin_d[L], 0, False),
                        (self.s_fwin[L], DFF + i * 512, n, f"fwin{L}", self.fwin_d[L], 0, False)])
            for i in range(4):
                blk(22, [(self.s_fwout[L], i * 256, 256, f"fwout{L}", self.fwout_d[L], 0, False)])
        self.wplan = plan
        self.wsem = [self.S_.new_sem(f"wslot{i}") for i in range(NSLOT)]

    def _issue_w(self, n):
        KC, segs = self.wplan[n % len(self.wplan)]
        slot = n % NSLOT
        o = 0
        first_tile = n < len(self.wplan)
        for (scr, c0, ncols, cname, src32, base, dup) in segs:
            dst = self.wring[:, slot, o:o + KC * ncols].rearrange("p (k n) -> p k n", k=KC)
            sview = scr[:, c0:c0 + ncols].rearrange("(k p) n -> p k n", p=128)
            if not first_tile:
                self.S_.dma("sp", self.wsem[slot], dst, sview, reads=["cv_" + cname], writes=[f"wslot{slot}"])
            else:
                if cname not in self.wkey:
                    self.wkey[cname] = self.S_.new_sem("cv_" + cname)
                if not dup:
                    src = src32[:, base + c0:base + c0 + ncols].rearrange("(k p) n -> p k n", p=128)
                    self.S_.dma("pool", self.wsem[slot], dst, src, writes=[f"wslot{slot}"])
                else:
                    d5 = dst.rearrange("p k (g u d) -> p k g u d", g=4, u=2)
                    for kc in range(KC):
                        src = src32[kc * 128:(kc + 1) * 128, base:base + 256].rearrange("p (g d) -> p g d", g=4)
                        for u in range(2):
                            self.S_.dma("pool", self.wsem[slot], d5[:, kc, :, u, :], src, writes=[f"wslot{slot}"])
                self.S_.dma("sp", self.wkey[cname], sview, dst, reads=[f"wslot{slot}"], writes=["cv_" + cname])
            o += KC * ncols

    def wacq(self):
        n = self.wnext_use
        total = len(self.wplan) * self.NT
        while self.wnext_issue < min(n + NSLOT, total):
            self._issue_w(self.wnext_issue)
            self.wnext_issue += 1
        self.wnext_use += 1
        KC, segs = self.wplan[n % len(self.wplan)]
        slot = n % NSLOT
        views, o = [], 0
        for (scr, c0, ncols, cname, src32, base, dup) in segs:
            views.append(self.wring[:, slot, o:o + KC * ncols].rearrange("p (k n) -> p k n", k=KC))
            o += KC * ncols
        return views, f"wslot{slot}"

    @staticmethod
    def pipeline(gens):
        active, it, done = [], iter(gens), False
        while True:
            if not done:
                nxt = next(it, None)
                if nxt is None:
                    done = True
                else:
                    active.append(nxt)
            if done and not active:
                break
            for g in list(active):
                try:
                    next(g)
                except StopIteration:
                    active.remove(g)

    def load_x(self, t):
        nc, S_ = self.nc, self.S_
        if not hasattr(self, "xsem"):
            self.xsem = S_.new_sem("xin")
            self.ysem = S_.new_sem("yout")
        xio = self.ybuf[:].rearrange("p a b -> p (a b)").rearrange("p (tb d) -> p tb d", tb=4)
        ykeys = [f"yb{c}" for c in range(8)]
        src = self.x_d[t * TT:(t + 1) * TT, :].rearrange("(tb p) d -> p tb d", p=128)
        S_.dma("sp", self.xsem, xio, src, writes=ykeys)
        ident = self.P("ident", 0, 128)

        def item(c):
            b = self.bank()
            for tb in range(4):
                S_.op("pe", lambda: nc.tensor.transpose(
                    self.ps[:, b, tb * 128:(tb + 1) * 128], xio[:, tb, c * 128:(c + 1) * 128], ident),
                    reads=ykeys + ["prm"], writes=[f"ps{b}"], inc=(tb == 3))
            yield
            S_.op("act", lambda: nc.scalar.copy(out=self.xT[:, c, :], in_=self.ps[:, b, :]),
                  reads=[f"ps{b}"], writes=[f"x{c}"])

        self.pipeline(item(c) for c in range(8))

    def store_x(self, t):
        nc, S_ = self.nc, self.S_
        xio = self.ybuf[:].rearrange("p a b -> p (a b)").rearrange("p (tb d) -> p tb d", tb=4)
        ykeys = [f"yb{c}" for c in range(8)]
        ident = self.P("ident", 0, 128)

        def item(tb, c4):
            b = self.bank()
            for ci in range(4):
                c = c4 * 4 + ci
                S_.op("pe", lambda: nc.tensor.transpose(
                    self.ps[:, b, ci * 128:(ci + 1) * 128], self.xT[:, c, tb * 128:(tb + 1) * 128], ident),
                    reads=[f"x{c}", "prm"], writes=[f"ps{b}"], inc=(ci == 3))
            yield
            S_.op("act", lambda: nc.scalar.copy(out=xio[:, tb, c4 * 512:(c4 + 1) * 512], in_=self.ps[:, b, :]),
                  reads=[f"ps{b}"], writes=[f"yb{tb * 2 + c4}"])

        self.pipeline(item(tb, c4) for tb in range(4) for c4 in range(2))
        dst = self.y_d[t * TT:(t + 1) * TT, :].rearrange("(tb p) d -> p tb d", p=128)
        S_.dma("sp", self.ysem, dst, xio, reads=ykeys, writes=["ydram"] + ykeys)

    def rstd_from_bank(self, b):
        nc, S_ = self.nc, self.S_
        import os
        if os.environ.get("K_RSTD") == "old":
            S_.op("act", lambda: nc.scalar.activation(out=self.nt1[:], in_=self.ps[:, b, :], func=AF.Sqrt,
                                                      bias=self.epsc[:], scale=1.0),
                  reads=[f"ps{b}", "epsc"], writes=["nt1"])
            S_.op("dve", lambda: nc.vector.reciprocal(out=self.ps[:, 7, :], in_=self.nt1[:]),
                  reads=["nt1"], writes=["ps7"])
            return
        S_.op("act", lambda: nc.scalar.activation(out=self.nt1[:], in_=self.ps[:, b, :], func=AF.Ln,
                                                  bias=self.epsc[:], scale=1.0),
              reads=[f"ps{b}", "epsc"], writes=["nt1"])
        S_.op("act", lambda: nc.scalar.activation(out=self.ps[:, 7, :], in_=self.nt1[:], func=AF.Exp, scale=-0.5),
              reads=["nt1"], writes=["ps7"])

    def prenorm(self, gname):
        nc, S_ = self.nc, self.S_
        for c in range(8):
            S_.op("act", lambda c=c: nc.scalar.activation(out=self.hT[:, c, :], in_=self.xT[:, c, :], func=AF.Square),
                  reads=[f"x{c}"], writes=[f"h{c}"])
        for c in range(8):
            self.mm(7, (0, TT), self.onesD[:], self.hT[:, c, :], c == 0, c == 7, [f"h{c}", "onesD"], c == 7)
        self.rstd_from_bank(7)
        for c in range(8):
            S_.op("dve", lambda c=c: nc.vector.scalar_tensor_tensor(
                out=self.hT[:, c, :], in0=self.xT[:, c, :], scalar=self.P(gname, c), in1=self.ps[:, 7, :],
                op0=ALU.mult, op1=ALU.mult), reads=[f"x{c}", "prm", "ps7"], writes=[f"h{c}"])

    def downproj_post(self, KC, rhs_fn, nblk, fac, bias_name, gname):
        nc, S_ = self.nc, self.S_
        dc_per = 8 // nblk
        st = {}
        S_.op("act", lambda: nc.scalar.activation(out=self.small[:, 32:33], in_=self.epsc[:], func=AF.Ln),
              reads=["epsc"], writes=["sm_dummy"])

        def item(dc):
            if dc % dc_per == 0:
                (st["wv"],), st["wk"] = self.wacq()
            wv, wk = st["wv"], st["wk"]
            dl = dc % dc_per
            b = self.bank()
            for kc in range(KC):
                rhs, rk = rhs_fn(kc)
                rk = rk if isinstance(rk, list) else [rk]
                self.mm(b, (0, TT), wv[:, kc, dl * 128:(dl + 1) * 128], rhs, kc == 0, kc == KC - 1, [wk] + rk, kc == KC - 1)
            yield
            bias = self.P(bias_name, dc) if bias_name else 0.0
            S_.op("act", lambda: nc.scalar.activation(
                out=self.ybuf[:, dc, :], in_=self.ps[:, b, :], func=AF.Identity, bias=bias, scale=fac),
                reads=[f"ps{b}", "prm"], writes=[f"yb{dc}"])
            S_.op("act", lambda: nc.scalar.activation(out=self.hT[:, dc, :], in_=self.ybuf[:, dc, :], func=AF.Square),
                  reads=[f"yb{dc}"], writes=[f"h{dc}"])
            yield
            self.mm(7, (0, TT), self.onesD[:], self.hT[:, dc, :], dc == 0, dc == 7, [f"h{dc}", "onesD"], dc == 7)

        self.pipeline(item(dc) for dc in range(8))
        self.rstd_from_bank(7)
        for dc in range(8):
            S_.op("dve", lambda dc=dc: nc.vector.scalar_tensor_tensor(
                out=self.ybuf[:, dc, :], in0=self.ybuf[:, dc, :], scalar=self.P(gname, dc), in1=self.ps[:, 7, :],
                op0=ALU.mult, op1=ALU.mult), reads=[f"yb{dc}", "prm", "ps7"], writes=[f"yb{dc}"])
            ae, aeng = ("dve", nc.vector) if dc in (3, 7) else ("pool", nc.gpsimd)
            S_.op(ae, lambda dc=dc, aeng=aeng: aeng.tensor_tensor(
                out=self.xT[:, dc, :], in0=self.xT[:, dc, :], in1=self.ybuf[:, dc, :], op=ALU.add),
                reads=[f"x{dc}", f"yb{dc}"], writes=[f"x{dc}"])

    def gelu_a(self, z_ap, zkeys, sq, sqk):
        nc, S_ = self.nc, self.S_
        S_.op("act", lambda: nc.scalar.activation(out=sq, in_=z_ap, func=AF.Square, scale=GA),
              reads=zkeys, writes=[sqk])
        S_.op("dve", lambda: nc.vector.scalar_tensor_tensor(out=sq, in0=sq, scalar=1.0, in1=z_ap,
                                                            op0=ALU.add, op1=ALU.mult),
              reads=zkeys + [sqk], writes=[sqk])

    def gelu_b(self, sq, sqk):
        nc, S_ = self.nc, self.S_
        S_.op("act", lambda: nc.scalar.activation(out=sq, in_=sq, func=AF.Tanh, scale=GC),
              reads=[sqk], writes=[sqk])

    def ffn(self, L, t):
        nc, S_ = self.nc, self.S_
        S_.fence("ar_")
        self.prenorm(f"nfp{L}")
        cwo = self.off[f"cw{L}"]
        st = {}

        def item(c):
            if c % 4 == 0:
                (st["gW"], st["uW"]), st["wk"] = self.wacq()
            gW, uW, wk = st["gW"], st["uW"], st["wk"]
            cl = c % 4
            i, i2 = c % 3, c % 2
            acc, sq, m = self.ybuf[:, i, :], self.ybuf[:, 3 + i, :], self.ybuf[:, 6 + i2, :]
            ka, ks, km = f"yb{i}", f"yb{3 + i}", f"yb{6 + i2}"
            bg = self.bank()
            for kc in range(8):
                self.mm(bg, (0, TT), gW[:, kc, cl * 128:(cl + 1) * 128], self.hT[:, kc, :], kc == 0, kc == 7, [wk, f"h{kc}"], kc == 7)
            bu = self.bank()
            for kc in range(8):
                self.mm(bu, (0, TT), uW[:, kc, cl * 128:(cl + 1) * 128], self.hT[:, kc, :], kc == 0, kc == 7, [wk, f"h{kc}"], kc == 7)
            yield
            w0 = self.prm[:, cwo + c * 3 + 0:cwo + c * 3 + 1]
            w1 = self.prm[:, cwo + c * 3 + 1:cwo + c * 3 + 2]
            w2 = self.prm[:, cwo + c * 3 + 2:cwo + c * 3 + 3]
            cb = self.P(f"cb{L}", c)
            pg = self.ps[:, bg, :]
            H = self.halo[:, L, c, :]
            S_.op("act", lambda: nc.scalar.activation(out=acc, in_=pg, func=AF.Identity, bias=cb, scale=w2),
                  reads=[f"ps{bg}", "prm"], writes=[ka])
            S_.op("dve", lambda: nc.vector.scalar_tensor_tensor(out=acc[:, 1:TT], in0=pg[:, 0:TT - 1], scalar=w1,
                                                                in1=acc[:, 1:TT], op0=ALU.mult, op1=ALU.add),
                  reads=[f"ps{bg}", "prm", ka], writes=[ka])
            S_.op("dve", lambda: nc.vector.scalar_tensor_tensor(out=acc[:, 2:TT], in0=pg[:, 0:TT - 2], scalar=w0,
                                                                in1=acc[:, 2:TT], op0=ALU.mult, op1=ALU.add),
                  reads=[f"ps{bg}", "prm", ka], writes=[ka])
            S_.op("dve", lambda: nc.vector.scalar_tensor_tensor(out=acc[:, 0:2], in0=H, scalar=w0,
                                                                in1=acc[:, 0:2], op0=ALU.mult, op1=ALU.add),
                  reads=[f"halo{c}", "prm", ka], writes=[ka])
            S_.op("dve", lambda: nc.vector.scalar_tensor_tensor(out=acc[:, 0:1], in0=H[:, 1:2], scalar=w1,
                                                                in1=acc[:, 0:1], op0=ALU.mult, op1=ALU.add),
                  reads=[f"halo{c}", "prm", ka], writes=[ka])
            S_.op("dve", lambda: nc.vector.tensor_copy(out=H, in_=pg[:, TT - 2:TT]),
                  reads=[f"ps{bg}"], writes=[f"halo{c}"])
            yield
            self.gelu_a(acc, [ka], sq, ks)
            yield
            self.gelu_b(sq, ks)
            S_.op("dve", lambda: nc.vector.scalar_tensor_tensor(out=m, in0=sq, scalar=1.0, in1=self.ps[:, bu, :],
                                                                op0=ALU.add, op1=ALU.mult),
                  reads=[ks, f"ps{bu}"], writes=[km])
            yield
            S_.op("pool", lambda: nc.gpsimd.tensor_tensor(out=self.gated[:, c, :], in0=m, in1=acc, op=ALU.mult),
                  reads=[km, ka], writes=[f"g{c}"])

        self.pipeline(item(c) for c in range(22))
        self.downproj_post(22, lambda kc: (self.gated[:, kc, :], f"g{kc}"), 4, 0.5, None, f"nfo{L}")

    def sgu(self, L, j, t):
        nc, S_ = self.nc, self.S_
        S_.fence("ar_")
        a = self.carve([("vbf", 4 * SH, BF16), ("Ct", 24 * 128, F32), ("stats", 4 * 6 * 6, F32), ("mv", 8, F32)])
        vbf = a["vbf"].rearrange("p (tc f) -> p tc f", tc=4)
        Ct = a["Ct"].rearrange("p (c t) -> p c t", c=24)
        stats = a["stats"].rearrange("p (tc n s) -> p tc n s", tc=4, n=6)
        mv = a["mv"].rearrange("p (tc s) -> p tc s", tc=4)
        self.prenorm(f"nmp{L}")
        for half in range(2):
            b = self.bank()
            for gi in range(4):
                g = half * 4 + gi
                self.mm(b, (gi * 128, (gi + 1) * 128), self.ones1[:], self.wsTm[:, j, g, :], True, True, ["ones1", "wsTm"], gi == 3)
            for gi in range(4):
                g = half * 4 + gi
                for cc in range(3):
                    c = g * 3 + cc
                    S_.op("dve", lambda c=c, g=g, gi=gi, b=b: nc.vector.scalar_tensor_tensor(
                        out=Ct[:, c, :], in0=self.ps[:, b, gi * 128:(gi + 1) * 128], scalar=self.P(f"lb{j}", c),
                        in1=self.P(f"bs{j}", g * 128, 128), op0=ALU.mult, op1=ALU.add),
                        reads=[f"ps{b}", "prm"], writes=["ar_Ct"])
        st = {}

        def vitem(k):
            nb, n2, tc = k // 8, (k // 4) % 2, k % 4
            if k % 8 == 0:
                (st["wv"],), st["wk"] = self.wacq()
            wv, wk = st["wv"], st["wk"]
            ntile = nb * 2 + n2
            i = k % 3
            sq, ks = self.ybuf[:, i, :], f"yb{i}"
            b = self.bank()
            for kc in range(8):
                self.mm(b, (0, TT), self.hT[:, kc, tc * 128:(tc + 1) * 128], wv[:, kc, n2 * 512:(n2 + 1) * 512],
                        kc == 0, kc == 7, [wk, f"h{kc}"], kc == 7)
            pz = self.ps[:, b, :]
            yield
            self.gelu_a(pz, [f"ps{b}"], sq, ks)
            yield
            self.gelu_b(sq, ks)
            vdst = vbf[:, tc, ntile * 512:(ntile + 1) * 512]
            S_.op("dve", lambda: nc.vector.scalar_tensor_tensor(out=vdst, in0=sq, scalar=1.0, in1=pz,
                                                                op0=ALU.add, op1=ALU.mult),
                  reads=[ks, f"ps{b}"], writes=[f"ar_v{tc}_{ntile}"])
            yield
            S_.op("dve", lambda: nc.vector.bn_stats(out=stats[:, tc, ntile, :], in_=vdst),
                  reads=[f"ar_v{tc}_{ntile}"], writes=[f"ar_stats{tc}_{ntile}"])

        self.pipeline(vitem(k) for k in range(24))
        sm = self.small
        for tc in range(4):
            S_.op("dve", lambda tc=tc: nc.vector.bn_aggr(out=mv[:, tc, :], in_=stats[:, tc].rearrange("p n s -> p (n s)")),
                  reads=[f"ar_stats{tc}_{n}" for n in range(6)], writes=["ar_mv"])
        S_.op("dve", lambda: nc.vector.tensor_scalar(out=sm[:, 0:4], in0=mv[:, :, 1], scalar1=0.25, scalar2=EPS,
                                                     op0=ALU.mult, op1=ALU.add), reads=["ar_mv"], writes=["sm0"])
        S_.op("pool", lambda: nc.gpsimd.tensor_tensor(out=sm[:, 4:8], in0=sm[:, 0:4], in1=self.cneg[:, 0:4], op=ALU.pow),
              reads=["sm0", "cneg"], writes=["sm1"])
        S_.op("dve", lambda: nc.vector.tensor_scalar(out=sm[:, 8:12], in0=sm[:, 4:8], scalar1=0.5, scalar2=None,
                                                     op0=ALU.mult), reads=["sm1"], writes=["sm2"])
        S_.op("dve", lambda: nc.vector.scalar_tensor_tensor(out=sm[:, 12:16], in0=mv[:, :, 0], scalar=-1.0, in1=sm[:, 8:12],
                                                            op0=ALU.mult, op1=ALU.mult), reads=["ar_mv", "sm2"], writes=["sm3"])
        for tc in range(4):
            for hf in range(2):
                seg = vbf[:, tc, hf * 1536:(hf + 1) * 1536]
                ks_ = [f"ar_v{tc}_{n}" for n in range(hf * 3, hf * 3 + 3)]
                if hf == 0:
                    S_.op("act", lambda tc=tc, seg=seg: nc.scalar.activation(
                        out=seg, in_=seg, func=AF.Identity, bias=sm[:, 12 + tc:13 + tc], scale=sm[:, 8 + tc:9 + tc]),
                        reads=ks_ + ["sm2", "sm3"], writes=ks_)
                else:
                    S_.op("dve", lambda tc=tc, seg=seg: nc.vector.tensor_scalar(
                        out=seg, in0=seg, scalar1=sm[:, 8 + tc:9 + tc], scalar2=sm[:, 12 + tc:13 + tc],
                        op0=ALU.mult, op1=ALU.add), reads=ks_ + ["sm2", "sm3"], writes=ks_)

        def uitem(c):
            if c % 8 == 0:
                (st["wu"],), st["wk"] = self.wacq()
            wu, wk = st["wu"], st["wk"]
            cl = c % 8
            g = c // 3
            i, i2 = c % 3, c % 2
            sq, ks = self.ybuf[:, i, :], f"yb{i}"
            ug, ku = self.ybuf[:, 3 + i2, :], f"yb{3 + i2}"
            sv, kv = self.ybuf[:, 5 + i, :], f"yb{5 + i}"
            bu = self.bank()
            for kc in range(8):
                self.mm(bu, (0, TT), wu[:, kc, cl * 128:(cl + 1) * 128], self.hT[:, kc, :], kc == 0, kc == 7, [wk, f"h{kc}"], kc == 7)
            pz = self.ps[:, bu, :]
            yield
            bs_ = self.bank()
            nt_ = (c * 128) // 512
            for tc in range(4):
                self.mm(bs_, (tc * 128, (tc + 1) * 128), vbf[:, tc, c * 128:(c + 1) * 128], self.wsTm[:, j, g, :], True, True,
                        [f"ar_v{tc}_{nt_}", "wsTm"], tc == 3)
            self.gelu_a(pz, [f"ps{bu}"], sq, ks)
            S_.op("dve", lambda: nc.vector.scalar_tensor_tensor(
                out=sv.rearrange("p (a t) -> p a t", a=4), in0=self.ps[:, bs_, :].rearrange("p (a t) -> p a t", a=4),
                scalar=self.P(f"lg{j}", c), in1=Ct[:, c, :].unsqueeze(1).broadcast_to([128, 4, 128]),
                op0=ALU.mult, op1=ALU.add), reads=[f"ps{bs_}", "prm", "ar_Ct"], writes=[kv])
            yield
            self.gelu_b(sq, ks)
            S_.op("dve", lambda: nc.vector.scalar_tensor_tensor(out=ug, in0=sq, scalar=1.0, in1=pz, op0=ALU.add, op1=ALU.mult),
                  reads=[ks, f"ps{bu}"], writes=[ku])
            yield
            S_.op("pool", lambda: nc.gpsimd.tensor_tensor(out=self.gated[:, c, :], in0=ug, in1=sv, op=ALU.mult),
                  reads=[ku, kv], writes=[f"g{c}"])

        self.pipeline(uitem(c) for c in range(24))
        self.downproj_post(24, lambda kc: (self.gated[:, kc, :], f"g{kc}"), 4, 0.5, None, f"nmo{L}")

    def attention(self, L, j, t):
        nc, S_ = self.nc, self.S_
        S_.fence("ar_")
        a = self.carve([("qT", 8 * TT, BF16), ("kTA", 4 * TT, BF16), ("kTB", 4 * TT, BF16), ("V", 4 * 512, BF16),
                        ("aT", 8 * TT, BF16), ("es16", 16, F32),
                        ("PX0", TT, BF16), ("PY0", TT, BF16), ("PX1", TT, BF16), ("PY1", TT, BF16),
                        ("lt", TT, F32), ("rl", TT, F32)])
        qT = a["qT"].rearrange("p (c t) -> p c t", c=8)
        kTA = a["kTA"].rearrange("p (g t) -> p g t", g=4)
        kTB = a["kTB"].rearrange("p (g t) -> p g t", g=4)
        S_.op("pool", lambda: nc.gpsimd.memset(a["kTA"][64:128, :], 0.0), writes=["ar_kz"])
        S_.op("pool", lambda: nc.gpsimd.memset(a["kTB"][0:64, :], 0.0), writes=["ar_kz"])
        V = a["V"].rearrange("p (b g u d) -> p b g u d", b=4, g=4, u=2)
        aT = a["aT"].rearrange("p (c t) -> p c t", c=8)
        self.prenorm(f"nmp{L}")
        S_.op("act", lambda: nc.scalar.activation(out=a["es16"], in_=self.P(f"sk{j}", 0, 16), func=AF.Exp),
              reads=["prm"], writes=["ar_es16"])
        st = {}

        def qitem(c):
            if c == 0:
                (st["wq"],), st["wkq"] = self.wacq()
            b = self.bank()
            for kc in range(8):
                self.mm(b, (0, TT), st["wq"][:, kc, c * 128:(c + 1) * 128], self.hT[:, kc, :], kc == 0, kc == 7, [st["wkq"], f"h{kc}"], kc == 7)
            yield
            S_.op("act", lambda: nc.scalar.activation(out=qT[:, c, :], in_=self.ps[:, b, :], func=AF.Identity,
                                                      bias=self.P(f"bq{j}", c), scale=1.0),
                  reads=[f"ps{b}", "prm"], writes=[f"ar_q{c}"])

        def kitem(g):
            if g == 0:
                (st["wkd"], st["wvv"]), st["wkk"] = self.wacq()
            b = self.bank()
            for kc in range(8):
                self.mm(b, (0, TT), st["wkd"][:, kc, g * 128:(g + 1) * 128], self.hT[:, kc, :], kc == 0, kc == 7, [st["wkk"], f"h{kc}"], kc == 7)
            yield
            S_.op("act", lambda: nc.scalar.activation(out=kTA[0:64, g, :], in_=self.ps[0:64, b, :], func=AF.Identity,
                                                      bias=self.P(f"bk{j}", g)[0:64, :], scale=1.0),
                  reads=[f"ps{b}", "prm"], writes=[f"ar_k{g}"])
            S_.op("act", lambda: nc.scalar.activation(out=kTB[64:128, g, :], in_=self.ps[64:128, b, :], func=AF.Identity,
                                                      bias=self.P(f"bk{j}", g)[64:128, :], scale=1.0),
                  reads=[f"ps{b}", "prm"], writes=[f"ar_k{g}"])

        bv3 = self.P(f"bv{j}", 0, 256).rearrange("p (g d) -> p g d", g=4)

        def vitem(tb):
            b = self.bank()
            for kc in range(8):
                self.mm(b, (0, 256), self.hT[:, kc, tb * 128:(tb + 1) * 128], st["wvv"][:, kc, :], kc == 0, kc == 7, [st["wkk"], f"h{kc}"], kc == 7)
            yield
            for u in range(2):
                S_.op("dve", lambda: nc.vector.tensor_tensor(
                    out=V[:, tb, :, u, :], in0=self.ps[:, b, 0:256].rearrange("p (g d) -> p g d", g=4), in1=bv3, op=ALU.add),
                    reads=[f"ps{b}", "prm"], writes=[f"ar_V{tb}"])

        self.pipeline([qitem(c) for c in range(8)] + [kitem(g) for g in range(4)] + [vitem(tb) for tb in range(4)])
        kcA, kcB = self.kcarry[:, j, 0], self.kcarry[:, j, 1]
        vc_ap = self.vcarry[:, j].rearrange("p (g u d) -> p g u d", g=4, u=2)
        ab = self.abias[:].rearrange("p h (xy g n) -> p h xy g n", xy=2, g=4)

        def aitem(n):
            qb, g = n // 4, n % 4
            has_prev = (t * 4 + qb) > 0
            qs = slice(qb * 128, (qb + 1) * 128)
            kbs = [0, 1] if has_prev else [1]
            c0 = kbs[0] * 256
            PXn, PYn = f"PX{n % 2}", f"PY{n % 2}"
            bX = self.bank()
            bY = self.bank()
            qk = [f"ar_q{2 * g}", f"ar_q{2 * g + 1}"]
            for i, kb in enumerate(kbs):
                if kb == 1:
                    kl0, kl1, kkey = kTA[:, g, qs], kTB[:, g, qs], f"ar_k{g}"
                elif qb > 0:
                    ps_ = slice((qb - 1) * 128, qb * 128)
                    kl0, kl1, kkey = kTA[:, g, ps_], kTB[:, g, ps_], f"ar_k{g}"
                else:
                    kl0, kl1, kkey = kcA[:, g, :], kcB[:, g, :], "kcarry"
                self.mm(bX, (kb * 256, kb * 256 + 256), kl0, qT[:, 2 * g:2 * g + 2, qs], i == 0, False, [kkey, "ar_kz"] + qk, False)
                self.mm(bY, (kb * 256, kb * 256 + 256), kl1, qT[:, 2 * g:2 * g + 2, qs], i == 0, False, [kkey, "ar_kz"] + qk, False)
            for (bb, xy) in ((bX, 0), (bY, 1)):
                for hl in range(2):
                    self.mm(bb, (c0, TT), self.identB[:], ab[:, hl, xy, g, c0:TT], False, hl == 1, ["identB", "abias"], hl == 1)
            yield
            for (bb, Pn) in ((bX, PXn), (bY, PYn)):
                S_.op("act", lambda: nc.scalar.activation(out=a[Pn][:, c0:TT], in_=self.ps[:, bb, c0:TT],
                                                          func=AF.Exp, scale=0.125),
                      reads=[f"ps{bb}"], writes=["ar_" + Pn])
            yield
            bo = self.bank()
            bl = self.bank()
            for (bank_, is_o) in ((bo, True), (bl, False)):
                for (xy, Pn) in ((0, PXn), (1, PYn)):
                    for ki, kb in enumerate(kbs):
                        if not is_o:
                            lhs, lk = self.ones1[:], "ones1"
                        elif kb == 1:
                            lhs, lk = V[:, qb, g].rearrange("p u d -> p (u d)"), f"ar_V{qb}"
                        elif qb > 0:
                            lhs, lk = V[:, qb - 1, g].rearrange("p u d -> p (u d)"), f"ar_V{qb - 1}"
                        else:
                            lhs, lk = vc_ap[:, g].rearrange("p u d -> p (u d)"), "vcarry"
                        self.mm(bank_, (xy * 256, xy * 256 + 256), lhs, a[Pn][:, kb * 256:kb * 256 + 256],
                                ki == 0, ki == len(kbs) - 1, [lk, "ar_" + Pn], ki == len(kbs) - 1)
            yield
            ltn = a["lt"][:, (n % 2) * 256:(n % 2) * 256 + 256]
            rln = a["rl"][:, (n % 2) * 256:(n % 2) * 256 + 256]
            kl_, kr_ = f"ar_lt{n % 2}", f"ar_rl{n % 2}"
            for hf in range(2):
                pr = slice(hf * 64, hf * 64 + 64)
                for hh in range(2):
                    S_.op("dve", lambda: nc.vector.tensor_scalar(
                        out=ltn[pr, hh * 128:(hh + 1) * 128],
                        in0=self.ps[pr, bl, hf * 256 + hh * 128:hf * 256 + (hh + 1) * 128],
                        scalar1=a["es16"][pr, 4 * g + 2 * hf + hh:4 * g + 2 * hf + hh + 1], scalar2=None, op0=ALU.add),
                        reads=[f"ps{bl}", "ar_es16"], writes=[f"{kl_}_{hf}{hh}"])
            S_.op("act", lambda: nc.scalar.activation(out=ltn, in_=ltn, func=AF.Ln),
                  reads=[f"{kl_}_{hf}{hh}" for hf in range(2) for hh in range(2)],
                  writes=[f"{kl_}_{hf}{hh}" for hf in range(2) for hh in range(2)])
            S_.op("act", lambda: nc.scalar.activation(out=rln, in_=ltn, func=AF.Exp, scale=-1.0),
                  reads=[f"{kl_}_{hf}{hh}" for hf in range(2) for hh in range(2)], writes=[kr_])
            yield
            for hf in range(2):
                pr = slice(hf * 64, hf * 64 + 64)
                cs = slice(hf * 256, hf * 256 + 256)
                S_.op("dve", lambda: nc.vector.tensor_tensor(
                    out=aT[pr, 2 * g:2 * g + 2, qs], in0=self.ps[pr, bo, cs].rearrange("p (a q) -> p a q", a=2),
                    in1=rln[pr, :].rearrange("p (a q) -> p a q", a=2), op=ALU.mult),
                    reads=[f"ps{bo}", kr_], writes=[f"ar_aT{2 * g}_{hf}", f"ar_aT{2 * g + 1}_{hf}"])

        self.nbank = 8
        self.pipeline(aitem(n) for n in range(16))
        self.nbank = 7
        self.pb = self.pb % 7
        S_.op("pool", lambda: nc.gpsimd.tensor_copy(out=kcA[0:64], in_=kTA[0:64, :, 3 * 128:4 * 128]),
              reads=[f"ar_k{g}" for g in range(4)], writes=["kcarry"])
        S_.op("pool", lambda: nc.gpsimd.tensor_copy(out=kcB[64:128], in_=kTB[64:128, :, 3 * 128:4 * 128]),
              reads=[f"ar_k{g}" for g in range(4)], writes=["kcarry"])
        S_.op("pool", lambda: nc.gpsimd.tensor_copy(out=self.vcarry[:, j], in_=V[:, 3].rearrange("p g u d -> p (g u d)")),
              reads=["ar_V3"], writes=["vcarry"])
        self.downproj_post(8, lambda kc: (aT[:, kc, :], [f"ar_aT{kc}_0", f"ar_aT{kc}_1"]), 1, 1.0, f"bo{j}", f"nmo{L}")


_CACHE = {}


def _get_nc(S, off, nprm, nlayers=4):
    key = (S, nprm, nlayers)
    if key not in _CACHE:
        _CACHE[key] = Builder(S, off, nprm, nlayers).build()
    return _CACHE[key]


def run(inputs, xs, nlayers=4):
    prm, off, wsT, maskT, bias = pack_params(inputs)
    S = xs[0].shape[0]
    nc = _get_nc(S, off, prm.shape[1], nlayers)
    f = lambda k: np.ascontiguousarray(np.asarray(inputs[k], np.float32))
    shared = {"prm": prm, "wsT": wsT, "maskT": maskT, "abias": bias,
              "attn_w_qkv": f("attn_w_qkv"), "attn_w_o": f("attn_w_o"),
              "sgu_w_in": f("sgu_w_in"), "sgu_w_out": f("sgu_w_out"),
              "ffn_w_in": f("ffn_w_in"), "ffn_w_out": f("ffn_w_out")}
    in_maps = [dict(shared, x=np.ascontiguousarray(np.asarray(xc, np.float32))) for xc in xs]
    res = run_bass_kernel_spmd(nc, in_maps, core_ids=list(range(len(xs))))
    return [np.asarray(r["y"], np.float32) for r in res.results]


def kernel(**inputs):
    x = np.asarray(inputs["x"], np.float32)
    ys = run(inputs, [x[b] for b in range(x.shape[0])])
    return np.stack(ys, axis=0).astype(np.float32)
```

```python
import numpy as np
from contextlib import ExitStack
import concourse.bass as bass
import concourse.mybir as mybir
from concourse.bass_utils import run_bass_kernel_spmd

F32 = mybir.dt.float32
BF16 = mybir.dt.bfloat16
AF = mybir.ActivationFunctionType
ALU = mybir.AluOpType

D = 1024
TT = 512
DFF = 2816
SH = 3072
EPS = 1e-6
GA = float(np.sqrt(0.044715))
GC = float(np.sqrt(2.0 / np.pi))
CLAMP = 480.0
NEG = -1.0e5
NSLOT = 3
SLOT_E = 8192


class Sync:
    def __init__(self, nc, ctx):
        self.nc = nc
        self.eng = {"pe": nc.tensor, "act": nc.scalar, "dve": nc.vector, "pool": nc.gpsimd, "sp": nc.sync}
        self.sems, self.count, self.ctx = {}, {}, ctx
        for k in self.eng:
            self.sems[k] = ctx.enter_context(nc.semaphore("sem_" + k))
            self.count[k] = 0
        self.waited = {k: {} for k in self.eng}
        self.last_w, self.readers = {}, {}

    def new_sem(self, name):
        key = "d_" + name
        self.sems[key] = self.ctx.enter_context(self.nc.semaphore(key))
        self.count[key] = 0
        return key

    def _wait(self, e, dep):
        sk, v = dep
        if sk == "pe" and e == "pe":
            return
        if self.waited[e].get(sk, 0) >= v:
            return
        self.eng[e].wait_ge(self.sems[sk], v)
        self.waited[e][sk] = v

    def _deps(self, e, reads, writes):
        for r in reads:
            if r in self.last_w:
                self._wait(e, self.last_w[r])
        for w in writes:
            if w in self.last_w:
                self._wait(e, self.last_w[w])
            for d in self.readers.get(w, ()):
                self._wait(e, d)

    def _commit(self, tag, reads, writes):
        for w in writes:
            self.last_w[w] = tag
            self.readers[w] = []
        for r in reads:
            self.readers.setdefault(r, []).append(tag)

    def op(self, e, ins_fn, reads=(), writes=(), inc=True):
        self._deps(e, reads, writes)
        ins = ins_fn()
        tag = (e, self.count[e] + 1)
        if inc:
            self.count[e] += 1
            ins.then_inc(self.sems[e], 1)
        self._commit(tag, reads, writes)
        return ins

    def dma(self, q, semkey, out, in_, reads=(), writes=()):
        self._deps(q, reads, writes)
        ins = self.eng[q].dma_start(out=out, in_=in_)
        self.count[semkey] += 16
        ins.then_inc(self.sems[semkey], 16)
        self._commit((semkey, self.count[semkey]), reads, writes)
        return ins

    def fence(self, prefix):
        deps = set()
        for k in list(self.last_w):
            if k.startswith(prefix):
                deps.add(self.last_w.pop(k))
        for k in list(self.readers):
            if k.startswith(prefix):
                deps.update(self.readers.pop(k))
        for e in ("pe", "act", "dve", "pool"):
            for d in deps:
                self._wait(e, d)

    def wait_final(self, e, keys):
        for b in keys:
            if b in self.last_w:
                self._wait(e, self.last_w[b])


def _colvec(v):
    v = np.asarray(v, np.float32)
    return np.ascontiguousarray(v.reshape(-1, 128).T)


def pack_params(inp):
    cols, off, cur = [], {}, 0

    def add(name, arr):
        nonlocal cur
        arr = np.asarray(arr, np.float32).reshape(128, -1)
        off[name] = cur
        cols.append(arr)
        cur += arr.shape[1]

    for L in range(4):
        add(f"nmp{L}", _colvec(inp["norm_mix_pre"][L]))
        add(f"nmo{L}", _colvec(inp["norm_mix_post"][L]))
        add(f"nfp{L}", _colvec(inp["norm_ffn_pre"][L]))
        add(f"nfo{L}", _colvec(inp["norm_ffn_post"][L]))
        cw = np.asarray(inp["ffn_conv_w"][L], np.float32)
        add(f"cw{L}", np.ascontiguousarray(cw.reshape(3, 22, 128).transpose(2, 1, 0)))
        add(f"cb{L}", _colvec(inp["ffn_conv_b"][L]))
    for j in range(2):
        bqkv = np.asarray(inp["attn_b_qkv"][j], np.float32)
        add(f"bq{j}", _colvec(bqkv[:1024]))
        bk = bqkv[1024:1280].reshape(4, 64)
        add(f"bk{j}", np.ascontiguousarray(np.concatenate([bk, bk], axis=1).T))
        add(f"bo{j}", _colvec(inp["attn_b_o"][j]))
        add(f"bv{j}", np.broadcast_to(bqkv[1280:1536][None, :], (128, 256)))
        sk = np.asarray(inp["attn_sinks"][j], np.float32)
        perm = [h for g in range(4) for h in (4 * g, 4 * g + 2, 4 * g + 1, 4 * g + 3)]
        add(f"sk{j}", np.broadcast_to(sk[perm][None, :], (128, 16)))
        add(f"lg{j}", _colvec(inp["sgu_ln_g"][j]))
        add(f"lb{j}", _colvec(inp["sgu_ln_b"][j]))
        add(f"bs{j}", np.broadcast_to(np.asarray(inp["sgu_b_s"][j], np.float32).reshape(1, 1024), (128, 1024)))
    add("ident", np.eye(128, dtype=np.float32))
    prm = np.ascontiguousarray(np.concatenate(cols, axis=1))
    ws = np.asarray(inp["sgu_w_s"], np.float32)
    wsT = np.ascontiguousarray(ws.transpose(0, 3, 1, 2).reshape(2, 128, 1024))
    s_i = np.arange(128)[:, None]
    t_i = np.arange(128)[None, :]
    mask = (s_i <= t_i).astype(np.float32)
    maskT = np.ascontiguousarray(np.tile(mask, (1, 8)))
    slopes = np.exp2(-8.0 * np.arange(1, 17, dtype=np.float64) / 16.0)
    k_i = np.arange(128)[:, None]
    q_i = np.arange(128)[None, :]
    bias = np.zeros((128, 2, 4, 2, 2, 128), np.float32)
    for xy in range(2):
        for g in range(4):
            for hh in range(2):
                h = 4 * g + 2 * hh + xy
                dist_c = (q_i - k_i).astype(np.float64)
                cur_b = np.where(k_i <= q_i, -slopes[h] * dist_c * 8.0, NEG)
                dist_p = (q_i + 128 - k_i).astype(np.float64)
                prv_b = np.where(k_i > q_i, -slopes[h] * dist_p * 8.0, NEG)
                bias[:, xy, g, 0, hh, :] = prv_b
                bias[:, xy, g, 1, hh, :] = cur_b
    bias = np.ascontiguousarray(bias.reshape(128, 4096))
    return prm, off, wsT, maskT, bias


class Builder:
    def __init__(self, S, off, nprm, nlayers=4):
        self.S, self.off, self.nprm, self.nlayers = S, off, nprm, nlayers
        self.NT = S // TT

    def bank(self):
        self.pb = (self.pb + 1) % self.nbank
        return self.pb

    def P(self, name, c=0, n=1):
        o = self.off[name] + c
        return self.prm[:, o:o + n]

    def mm(self, b, cols, lhsT, rhs, start, stop, reads, last):
        nc = self.nc
        out = self.ps[:, b, cols[0]:cols[1]]
        self.S_.op("pe", lambda: nc.tensor.matmul(out, lhsT=lhsT, rhs=rhs, start=start, stop=stop),
                   reads=reads, writes=[f"ps{b}"], inc=last)

    def build(self):
        nc = bass.Bass("TRN2", target_bir_lowering=False)
        self.nc = nc
        S = self.S
        dt = nc.dram_tensor
        self.x_d = dt("x", [S, D], F32, kind="ExternalInput").ap()
        self.prm_d = dt("prm", [128, self.nprm], F32, kind="ExternalInput").ap()
        self.wsT_d = dt("wsT", [2, 128, 1024], F32, kind="ExternalInput").ap()
        self.mask_d = dt("maskT", [128, 1024], F32, kind="ExternalInput").ap()
        self.bias_d = dt("abias", [128, 4096], F32, kind="ExternalInput").ap()
        self.wqkv_d = dt("attn_w_qkv", [2, D, 1536], F32, kind="ExternalInput").ap()
        self.wo_d = dt("attn_w_o", [2, D, D], F32, kind="ExternalInput").ap()
        self.swin_d = dt("sgu_w_in", [2, D, 2 * SH], F32, kind="ExternalInput").ap()
        self.swout_d = dt("sgu_w_out", [2, SH, D], F32, kind="ExternalInput").ap()
        self.fwin_d = dt("ffn_w_in", [4, D, 2 * DFF], F32, kind="ExternalInput").ap()
        self.fwout_d = dt("ffn_w_out", [4, DFF, D], F32, kind="ExternalInput").ap()
        self.y_d = dt("y", [S, D], F32, kind="ExternalOutput").ap()
        self.s_wq = [dt(f"s_wq{j}", [D, 1024], BF16, kind="Internal").ap() for j in range(2)]
        self.s_wkd = [dt(f"s_wkd{j}", [D, 512], BF16, kind="Internal").ap() for j in range(2)]
        self.s_wv = [dt(f"s_wv{j}", [D, 256], BF16, kind="Internal").ap() for j in range(2)]
        self.s_wo = [dt(f"s_wo{j}", [D, D], BF16, kind="Internal").ap() for j in range(2)]
        self.s_swin = [dt(f"s_swin{j}", [D, 2 * SH], BF16, kind="Internal").ap() for j in range(2)]
        self.s_swout = [dt(f"s_swout{j}", [SH, D], BF16, kind="Internal").ap() for j in range(2)]
        self.s_fwin = [dt(f"s_fwin{L}", [D, 2 * DFF], BF16, kind="Internal").ap() for L in range(4)]
        self.s_fwout = [dt(f"s_fwout{L}", [DFF, D], BF16, kind="Internal").ap() for L in range(4)]

        with ExitStack() as ctx:
            self.S_ = Sync(nc, ctx)
            sb = lambda name, shape, dtp: ctx.enter_context(nc.sbuf_tensor(name, shape, dtp))
            self.prm = sb("prm_sb", [128, self.nprm], F32)
            self.abias = sb("abias_sb", [128, 2, 4096], BF16)
            self.identB = sb("identB", [128, 128], BF16)
            self.xT = sb("xT", [128, 8, TT], F32)
            self.hT = sb("hT", [128, 8, TT], BF16)
            self.gated = sb("gated", [128, 24, TT], BF16)
            self.ybuf = sb("ybuf", [128, 8, TT], F32)
            self.wring = sb("wring", [128, NSLOT, SLOT_E], BF16)
            self.onesD = sb("onesD", [128, 128], BF16)
            self.ones1 = sb("ones1", [128, 128], BF16)
            self.cneg = sb("cneg", [128, 8], F32)
            self.epsc = sb("epsc", [128, 1], F32)
            self.nt1 = sb("nt1", [128, TT], F32)
            self.rstd = sb("rstd", [128, TT], F32)
            self.halo = sb("halo", [128, 4, 22, 2], F32)
            self.wsTm = sb("wsTm", [128, 2, 8, 128], BF16)
            self.kcarry = sb("kcarry", [128, 2, 2, 4, 128], BF16)
            self.vcarry = sb("vcarry", [128, 2, 512], BF16)
            self.small = sb("small", [128, 64], F32)
            self.ARENA = 9400
            self.arena = sb("arena", [128, self.ARENA], F32)
            self.ps = ctx.enter_context(nc.psum_tensor("ps", [128, 8, TT], F32))
            self.pb = 0
            self.nbank = 7
            self.prologue()
            self.make_wplan()
            self.wnext_issue = 0
            self.wnext_use = 0
            for t in range(self.NT):
                self.load_x(t)
                for L in range(self.nlayers):
                    if L % 2 == 0:
                        self.attention(L, L // 2, t)
                    else:
                        self.sgu(L, L // 2, t)
                    self.ffn(L, t)
                self.store_x(t)
            self.S_.wait_final("sp", ["ydram"])
        return nc

    def carve(self, specs):
        out, o = {}, 0
        for name, n, dtp in specs:
            ncol = n if dtp == F32 else (n + 1) // 2
            ap = self.arena[:, o:o + ncol]
            if dtp != F32:
                ap = ap.bitcast(dtp)
            out[name] = ap
            o += ncol
        assert o <= self.ARENA, (o, self.ARENA)
        return out

    def prologue(self):
        nc, S_ = self.nc, self.S_
        d = S_.new_sem("prm")
        S_.dma("sp", d, self.prm[:], self.prm_d[:, :], writes=["prm"])
        a = self.carve([("ab", 4096, F32), ("ws", 2048, F32), ("mk", 1024, F32)])
        ab32 = a["ab"]
        d = S_.new_sem("abias")
        S_.dma("sp", d, ab32, self.bias_d[:, :], writes=["ar_ab"])
        S_.op("dve", lambda: nc.vector.tensor_copy(out=self.abias[:, 0, :], in_=ab32), reads=["ar_ab"], writes=["abias"])
        S_.op("dve", lambda: nc.vector.tensor_tensor(out=self.abias[:, 1, :], in0=ab32, in1=self.abias[:, 0, :], op=ALU.subtract),
              reads=["ar_ab", "abias"], writes=["abias"])
        S_.op("dve", lambda: nc.vector.tensor_copy(out=self.identB[:], in_=self.P("ident", 0, 128)), reads=["prm"], writes=["identB"])
        S_.fence("ar_")
        S_.op("dve", lambda: nc.vector.memset(self.onesD[:], 1.0 / D), writes=["onesD"])
        S_.op("dve", lambda: nc.vector.memset(self.ones1[:], 1.0), writes=["ones1"])
        S_.op("dve", lambda: nc.vector.memset(self.cneg[:], -0.5), writes=["cneg"])
        S_.op("dve", lambda: nc.vector.memset(self.epsc[:], EPS), writes=["epsc"])
        S_.op("dve", lambda: nc.vector.memset(self.halo[:].rearrange("p a b c -> p (a b c)"), 0.0), writes=[f"halo{c}" for c in range(22)])
        S_.op("dve", lambda: nc.vector.memset(self.kcarry[:].rearrange("p a e b c -> p (a e b c)"), 0.0), writes=["kcarry"])
        S_.op("dve", lambda: nc.vector.memset(self.vcarry[:].rearrange("p a b -> p (a b)"), 0.0), writes=["vcarry"])
        d = S_.new_sem("ws")
        for j in range(2):
            S_.dma("sp", d, a["ws"][:, j * 1024:(j + 1) * 1024], self.wsT_d[j], writes=["ar_ws"])
        S_.dma("sp", d, a["mk"], self.mask_d[:, :], writes=["ar_mk"])
        for j in range(2):
            S_.op("dve", lambda j=j: nc.vector.tensor_tensor(
                out=self.wsTm[:, j].rearrange("p g t -> p (g t)"), in0=a["ws"][:, j * 1024:(j + 1) * 1024],
                in1=a["mk"], op=ALU.mult), reads=["ar_ws", "ar_mk"], writes=["wsTm"])
        S_.fence("ar_")
        self.wkey = {}

    def make_wplan(self):
        plan = []

        def blk(KC, segs):
            assert sum(KC * s[2] for s in segs) <= SLOT_E
            plan.append((KC, segs))

        for L in range(self.nlayers):
            j = L // 2
            if L % 2 == 0:
                blk(8, [(self.s_wq[j], 0, 1024, f"wq{j}", self.wqkv_d[j], 0, False)])
                blk(8, [(self.s_wkd[j], 0, 512, f"wkd{j}", self.wqkv_d[j], 1024, True),
                        (self.s_wv[j], 0, 256, f"wv{j}", self.wqkv_d[j], 1280, False)])
                blk(8, [(self.s_wo[j], 0, 1024, f"wo{j}", self.wo_d[j], 0, False)])
            else:
                for i in range(3):
                    blk(8, [(self.s_swin[j], SH + i * 1024, 1024, f"swin{j}", self.swin_d[j], 0, False)])
                for i in range(3):
                    blk(8, [(self.s_swin[j], i * 1024, 1024, f"swin{j}", self.swin_d[j], 0, False)])
                for i in range(4):
                    blk(24, [(self.s_swout[j], i * 256, 256, f"swout{j}", self.swout_d[j], 0, False)])
            for i in range(6):
                n = 512 if i < 5 else 256
                blk(8, [(self.s_fwin[L], i * 512, n, f"fwin{L}", self.fwin_d[L], 0, False),
                        (self.s_fwin[L], DFF + i * 512, n, f"fwin{L}", self.fwin_d[L], 0, False)])
            for i in range(4):
                blk(22, [(self.s_fwout[L], i * 256, 256, f"fwout{L}", self.fwout_d[L], 0, False)])
        self.wplan = plan
        self.wsem = [self.S_.new_sem(f"wslot{i}") for i in range(NSLOT)]

    def _issue_w(self, n):
        KC, segs = self.wplan[n % len(self.wplan)]
        slot = n % NSLOT
        o = 0
        first_tile = n < len(self.wplan)
        for (scr, c0, ncols, cname, src32, base, dup) in segs:
            dst = self.wring[:, slot, o:o + KC * ncols].rearrange("p (k n) -> p k n", k=KC)
            sview = scr[:, c0:c0 + ncols].rearrange("(k p) n -> p k n", p=128)
            if not first_tile:
                self.S_.dma("sp", self.wsem[slot], dst, sview, reads=["cv_" + cname], writes=[f"wslot{slot}"])
            else:
                if cname not in self.wkey:
                    self.wkey[cname] = self.S_.new_sem("cv_" + cname)
                if not dup:
                    src = src32[:, base + c0:base + c0 + ncols].rearrange("(k p) n -> p k n", p=128)
                    self.S_.dma("pool", self.wsem[slot], dst, src, writes=[f"wslot{slot}"])
                else:
                    d5 = dst.rearrange("p k (g u d) -> p k g u d", g=4, u=2)
                    for kc in range(KC):
                        src = src32[kc * 128:(kc + 1) * 128, base:base + 256].rearrange("p (g d) -> p g d", g=4)
                        for u in range(2):
                            self.S_.dma("pool", self.wsem[slot], d5[:, kc, :, u, :], src, writes=[f"wslot{slot}"])
                self.S_.dma("sp", self.wkey[cname], sview, dst, reads=[f"wslot{slot}"], writes=["cv_" + cname])
            o += KC * ncols

    def wacq(self):
        n = self.wnext_use
        total = len(self.wplan) * self.NT
        while self.wnext_issue < min(n + NSLOT, total):
            self._issue_w(self.wnext_issue)
            self.wnext_issue += 1
        self.wnext_use += 1
        KC, segs = self.wplan[n % len(self.wplan)]
        slot = n % NSLOT
        views, o = [], 0
        for (scr, c0, ncols, cname, src32, base, dup) in segs:
            views.append(self.wring[:, slot, o:o + KC * ncols].rearrange("p (k n) -> p k n", k=KC))
            o += KC * ncols
        return views, f"wslot{slot}"

    @staticmethod
    def pipeline(gens):
        active, it, done = [], iter(gens), False
        while True:
            if not done:
                nxt = next(it, None)
                if nxt is None:
                    done = True
                else:
                    active.append(nxt)
            if done and not active:
                break
            for g in list(active):
                try:
                    next(g)
                except StopIteration:
                    active.remove(g)

    def load_x(self, t):
        nc, S_ = self.nc, self.S_
        if not hasattr(self, "xsem"):
            self.xsem = S_.new_sem("xin")
            self.ysem = S_.new_sem("yout")
        xio = self.ybuf[:].rearrange("p a b -> p (a b)").rearrange("p (tb d) -> p tb d", tb=4)
        ykeys = [f"yb{c}" for c in range(8)]
        src = self.x_d[t * TT:(t + 1) * TT, :].rearrange("(tb p) d -> p tb d", p=128)
        S_.dma("sp", self.xsem, xio, src, writes=ykeys)
        ident = self.P("ident", 0, 128)

        def item(c):
            b = self.bank()
            for tb in range(4):
                S_.op("pe", lambda: nc.tensor.transpose(
                    self.ps[:, b, tb * 128:(tb + 1) * 128], xio[:, tb, c * 128:(c + 1) * 128], ident),
                    reads=ykeys + ["prm"], writes=[f"ps{b}"], inc=(tb == 3))
            yield
            S_.op("act", lambda: nc.scalar.copy(out=self.xT[:, c, :], in_=self.ps[:, b, :]),
                  reads=[f"ps{b}"], writes=[f"x{c}"])

        self.pipeline(item(c) for c in range(8))

    def store_x(self, t):
        nc, S_ = self.nc, self.S_
        xio = self.ybuf[:].rearrange("p a b -> p (a b)").rearrange("p (tb d) -> p tb d", tb=4)
        ykeys = [f"yb{c}" for c in range(8)]
        ident = self.P("ident", 0, 128)

        def item(tb, c4):
            b = self.bank()
            for ci in range(4):
                c = c4 * 4 + ci
                S_.op("pe", lambda: nc.tensor.transpose(
                    self.ps[:, b, ci * 128:(ci + 1) * 128], self.xT[:, c, tb * 128:(tb + 1) * 128], ident),
                    reads=[f"x{c}", "prm"], writes=[f"ps{b}"], inc=(ci == 3))
            yield
            S_.op("act", lambda: nc.scalar.copy(out=xio[:, tb, c4 * 512:(c4 + 1) * 512], in_=self.ps[:, b, :]),
                  reads=[f"ps{b}"], writes=[f"yb{tb * 2 + c4}"])

        self.pipeline(item(tb, c4) for tb in range(4) for c4 in range(2))
        dst = self.y_d[t * TT:(t + 1) * TT, :].rearrange("(tb p) d -> p tb d", p=128)
        S_.dma("sp", self.ysem, dst, xio, reads=ykeys, writes=["ydram"] + ykeys)

    def rstd_from_bank(self, b):
        nc, S_ = self.nc, self.S_
        import os
        if os.environ.get("K_RSTD") == "old":
            S_.op("act", lambda: nc.scalar.activation(out=self.nt1[:], in_=self.ps[:, b, :], func=AF.Sqrt,
                                                      bias=self.epsc[:], scale=1.0),
                  reads=[f"ps{b}", "epsc"], writes=["nt1"])
            S_.op("dve", lambda: nc.vector.reciprocal(out=self.ps[:, 7, :], in_=self.nt1[:]),
                  reads=["nt1"], writes=["ps7"])
            return
        S_.op("act", lambda: nc.scalar.activation(out=self.nt1[:], in_=self.ps[:, b, :], func=AF.Ln,
                                                  bias=self.epsc[:], scale=1.0),
              reads=[f"ps{b}", "epsc"], writes=["nt1"])
        S_.op("act", lambda: nc.scalar.activation(out=self.ps[:, 7, :], in_=self.nt1[:], func=AF.Exp, scale=-0.5),
              reads=["nt1"], writes=["ps7"])

    def prenorm(self, gname):
        nc, S_ = self.nc, self.S_
        for c in range(8):
            S_.op("act", lambda c=c: nc.scalar.activation(out=self.hT[:, c, :], in_=self.xT[:, c, :], func=AF.Square),
                  reads=[f"x{c}"], writes=[f"h{c}"])
        for c in range(8):
            self.mm(7, (0, TT), self.onesD[:], self.hT[:, c, :], c == 0, c == 7, [f"h{c}", "onesD"], c == 7)
        self.rstd_from_bank(7)
        for c in range(8):
            S_.op("dve", lambda c=c: nc.vector.scalar_tensor_tensor(
                out=self.hT[:, c, :], in0=self.xT[:, c, :], scalar=self.P(gname, c), in1=self.ps[:, 7, :],
                op0=ALU.mult, op1=ALU.mult), reads=[f"x{c}", "prm", "ps7"], writes=[f"h{c}"])

    def downproj_post(self, KC, rhs_fn, nblk, fac, bias_name, gname):
        nc, S_ = self.nc, self.S_
        dc_per = 8 // nblk
        st = {}
        S_.op("act", lambda: nc.scalar.activation(out=self.small[:, 32:33], in_=self.epsc[:], func=AF.Ln),
              reads=["epsc"], writes=["sm_dummy"])

        def item(dc):
            if dc % dc_per == 0:
                (st["wv"],), st["wk"] = self.wacq()
            wv, wk = st["wv"], st["wk"]
            dl = dc % dc_per
            b = self.bank()
            for kc in range(KC):
                rhs, rk = rhs_fn(kc)
                rk = rk if isinstance(rk, list) else [rk]
                self.mm(b, (0, TT), wv[:, kc, dl * 128:(dl + 1) * 128], rhs, kc == 0, kc == KC - 1, [wk] + rk, kc == KC - 1)
            yield
            bias = self.P(bias_name, dc) if bias_name else 0.0
            S_.op("act", lambda: nc.scalar.activation(
                out=self.ybuf[:, dc, :], in_=self.ps[:, b, :], func=AF.Identity, bias=bias, scale=fac),
                reads=[f"ps{b}", "prm"], writes=[f"yb{dc}"])
            S_.op("act", lambda: nc.scalar.activation(out=self.hT[:, dc, :], in_=self.ybuf[:, dc, :], func=AF.Square),
                  reads=[f"yb{dc}"], writes=[f"h{dc}"])
            yield
            self.mm(7, (0, TT), self.onesD[:], self.hT[:, dc, :], dc == 0, dc == 7, [f"h{dc}", "onesD"], dc == 7)

        self.pipeline(item(dc) for dc in range(8))
        self.rstd_from_bank(7)
        for dc in range(8):
            S_.op("dve", lambda dc=dc: nc.vector.scalar_tensor_tensor(
                out=self.ybuf[:, dc, :], in0=self.ybuf[:, dc, :], scalar=self.P(gname, dc), in1=self.ps[:, 7, :],
                op0=ALU.mult, op1=ALU.mult), reads=[f"yb{dc}", "prm", "ps7"], writes=[f"yb{dc}"])
            ae, aeng = ("dve", nc.vector) if dc in (3, 7) else ("pool", nc.gpsimd)
            S_.op(ae, lambda dc=dc, aeng=aeng: aeng.tensor_tensor(
                out=self.xT[:, dc, :], in0=self.xT[:, dc, :], in1=self.ybuf[:, dc, :], op=ALU.add),
                reads=[f"x{dc}", f"yb{dc}"], writes=[f"x{dc}"])

    def gelu_a(self, z_ap, zkeys, sq, sqk):
        nc, S_ = self.nc, self.S_
        S_.op("act", lambda: nc.scalar.activation(out=sq, in_=z_ap, func=AF.Square, scale=GA),
              reads=zkeys, writes=[sqk])
        S_.op("dve", lambda: nc.vector.scalar_tensor_tensor(out=sq, in0=sq, scalar=1.0, in1=z_ap,
                                                            op0=ALU.add, op1=ALU.mult),
              reads=zkeys + [sqk], writes=[sqk])

    def gelu_b(self, sq, sqk):
        nc, S_ = self.nc, self.S_
        S_.op("act", lambda: nc.scalar.activation(out=sq, in_=sq, func=AF.Tanh, scale=GC),
              reads=[sqk], writes=[sqk])

    def ffn(self, L, t):
        nc, S_ = self.nc, self.S_
        S_.fence("ar_")
        self.prenorm(f"nfp{L}")
        cwo = self.off[f"cw{L}"]
        st = {}

        def item(c):
            if c % 4 == 0:
                (st["gW"], st["uW"]), st["wk"] = self.wacq()
            gW, uW, wk = st["gW"], st["uW"], st["wk"]
            cl = c % 4
            i, i2 = c % 3, c % 2
            acc, sq, m = self.ybuf[:, i, :], self.ybuf[:, 3 + i, :], self.ybuf[:, 6 + i2, :]
            ka, ks, km = f"yb{i}", f"yb{3 + i}", f"yb{6 + i2}"
            bg = self.bank()
            for kc in range(8):
                self.mm(bg, (0, TT), gW[:, kc, cl * 128:(cl + 1) * 128], self.hT[:, kc, :], kc == 0, kc == 7, [wk, f"h{kc}"], kc == 7)
            bu = self.bank()
            for kc in range(8):
                self.mm(bu, (0, TT), uW[:, kc, cl * 128:(cl + 1) * 128], self.hT[:, kc, :], kc == 0, kc == 7, [wk, f"h{kc}"], kc == 7)
            yield
            w0 = self.prm[:, cwo + c * 3 + 0:cwo + c * 3 + 1]
            w1 = self.prm[:, cwo + c * 3 + 1:cwo + c * 3 + 2]
            w2 = self.prm[:, cwo + c * 3 + 2:cwo + c * 3 + 3]
            cb = self.P(f"cb{L}", c)
            pg = self.ps[:, bg, :]
            H = self.halo[:, L, c, :]
            S_.op("act", lambda: nc.scalar.activation(out=acc, in_=pg, func=AF.Identity, bias=cb, scale=w2),
                  reads=[f"ps{bg}", "prm"], writes=[ka])
            S_.op("dve", lambda: nc.vector.scalar_tensor_tensor(out=acc[:, 1:TT], in0=pg[:, 0:TT - 1], scalar=w1,
                                                                in1=acc[:, 1:TT], op0=ALU.mult, op1=ALU.add),
                  reads=[f"ps{bg}", "prm", ka], writes=[ka])
            S_.op("dve", lambda: nc.vector.scalar_tensor_tensor(out=acc[:, 2:TT], in0=pg[:, 0:TT - 2], scalar=w0,
                                                                in1=acc[:, 2:TT], op0=ALU.mult, op1=ALU.add),
                  reads=[f"ps{bg}", "prm", ka], writes=[ka])
            S_.op("dve", lambda: nc.vector.scalar_tensor_tensor(out=acc[:, 0:2], in0=H, scalar=w0,
                                                                in1=acc[:, 0:2], op0=ALU.mult, op1=ALU.add),
                  reads=[f"halo{c}", "prm", ka], writes=[ka])
            S_.op("dve", lambda: nc.vector.scalar_tensor_tensor(out=acc[:, 0:1], in0=H[:, 1:2], scalar=w1,
                                                                in1=acc[:, 0:1], op0=ALU.mult, op1=ALU.add),
                  reads=[f"halo{c}", "prm", ka], writes=[ka])
            S_.op("dve", lambda: nc.vector.tensor_copy(out=H, in_=pg[:, TT - 2:TT]),
                  reads=[f"ps{bg}"], writes=[f"halo{c}"])
            yield
            self.gelu_a(acc, [ka], sq, ks)
            yield
            self.gelu_b(sq, ks)
            S_.op("dve", lambda: nc.vector.scalar_tensor_tensor(out=m, in0=sq, scalar=1.0, in1=self.ps[:, bu, :],
                                                                op0=ALU.add, op1=ALU.mult),
                  reads=[ks, f"ps{bu}"], writes=[km])
            yield
            S_.op("pool", lambda: nc.gpsimd.tensor_tensor(out=self.gated[:, c, :], in0=m, in1=acc, op=ALU.mult),
                  reads=[km, ka], writes=[f"g{c}"])

        self.pipeline(item(c) for c in range(22))
        self.downproj_post(22, lambda kc: (self.gated[:, kc, :], f"g{kc}"), 4, 0.5, None, f"nfo{L}")

    def sgu(self, L, j, t):
        nc, S_ = self.nc, self.S_
        S_.fence("ar_")
        a = self.carve([("vbf", 4 * SH, BF16), ("Ct", 24 * 128, F32), ("stats", 4 * 6 * 6, F32), ("mv", 8, F32)])
        vbf = a["vbf"].rearrange("p (tc f) -> p tc f", tc=4)
        Ct = a["Ct"].rearrange("p (c t) -> p c t", c=24)
        stats = a["stats"].rearrange("p (tc n s) -> p tc n s", tc=4, n=6)
        mv = a["mv"].rearrange("p (tc s) -> p tc s", tc=4)
        self.prenorm(f"nmp{L}")
        for half in range(2):
            b = self.bank()
            for gi in range(4):
                g = half * 4 + gi
                self.mm(b, (gi * 128, (gi + 1) * 128), self.ones1[:], self.wsTm[:, j, g, :], True, True, ["ones1", "wsTm"], gi == 3)
            for gi in range(4):
                g = half * 4 + gi
                for cc in range(3):
                    c = g * 3 + cc
                    S_.op("dve", lambda c=c, g=g, gi=gi, b=b: nc.vector.scalar_tensor_tensor(
                        out=Ct[:, c, :], in0=self.ps[:, b, gi * 128:(gi + 1) * 128], scalar=self.P(f"lb{j}", c),
                        in1=self.P(f"bs{j}", g * 128, 128), op0=ALU.mult, op1=ALU.add),
                        reads=[f"ps{b}", "prm"], writes=["ar_Ct"])
        st = {}

        def vitem(k):
            nb, n2, tc = k // 8, (k // 4) % 2, k % 4
            if k % 8 == 0:
                (st["wv"],), st["wk"] = self.wacq()
            wv, wk = st["wv"], st["wk"]
            ntile = nb * 2 + n2
            i = k % 3
            sq, ks = self.ybuf[:, i, :], f"yb{i}"
            b = self.bank()
            for kc in range(8):
                self.mm(b, (0, TT), self.hT[:, kc, tc * 128:(tc + 1) * 128], wv[:, kc, n2 * 512:(n2 + 1) * 512],
                        kc == 0, kc == 7, [wk, f"h{kc}"], kc == 7)
            pz = self.ps[:, b, :]
            yield
            self.gelu_a(pz, [f"ps{b}"], sq, ks)
            yield
            self.gelu_b(sq, ks)
            vdst = vbf[:, tc, ntile * 512:(ntile + 1) * 512]
            S_.op("dve", lambda: nc.vector.scalar_tensor_tensor(out=vdst, in0=sq, scalar=1.0, in1=pz,
                                                                op0=ALU.add, op1=ALU.mult),
                  reads=[ks, f"ps{b}"], writes=[f"ar_v{tc}_{ntile}"])
            yield
            S_.op("dve", lambda: nc.vector.bn_stats(out=stats[:, tc, ntile, :], in_=vdst),
                  reads=[f"ar_v{tc}_{ntile}"], writes=[f"ar_stats{tc}_{ntile}"])

        self.pipeline(vitem(k) for k in range(24))
        sm = self.small
        for tc in range(4):
            S_.op("dve", lambda tc=tc: nc.vector.bn_aggr(out=mv[:, tc, :], in_=stats[:, tc].rearrange("p n s -> p (n s)")),
                  reads=[f"ar_stats{tc}_{n}" for n in range(6)], writes=["ar_mv"])
        S_.op("dve", lambda: nc.vector.tensor_scalar(out=sm[:, 0:4], in0=mv[:, :, 1], scalar1=0.25, scalar2=EPS,
                                                     op0=ALU.mult, op1=ALU.add), reads=["ar_mv"], writes=["sm0"])
        S_.op("pool", lambda: nc.gpsimd.tensor_tensor(out=sm[:, 4:8], in0=sm[:, 0:4], in1=self.cneg[:, 0:4], op=ALU.pow),
              reads=["sm0", "cneg"], writes=["sm1"])
        S_.op("dve", lambda: nc.vector.tensor_scalar(out=sm[:, 8:12], in0=sm[:, 4:8], scalar1=0.5, scalar2=None,
                                                     op0=ALU.mult), reads=["sm1"], writes=["sm2"])
        S_.op("dve", lambda: nc.vector.scalar_tensor_tensor(out=sm[:, 12:16], in0=mv[:, :, 0], scalar=-1.0, in1=sm[:, 8:12],
                                                            op0=ALU.mult, op1=ALU.mult), reads=["ar_mv", "sm2"], writes=["sm3"])
        for tc in range(4):
            for hf in range(2):
                seg = vbf[:, tc, hf * 1536:(hf + 1) * 1536]
                ks_ = [f"ar_v{tc}_{n}" for n in range(hf * 3, hf * 3 + 3)]
                if hf == 1:
                    S_.op("act", lambda tc=tc, seg=seg: nc.scalar.activation(
                        out=seg, in_=seg, func=AF.Identity, bias=sm[:, 12 + tc:13 + tc], scale=sm[:, 8 + tc:9 + tc]),
                        reads=ks_ + ["sm2", "sm3"], writes=ks_)
                else:
                    S_.op("dve", lambda tc=tc, seg=seg: nc.vector.tensor_scalar(
                        out=seg, in0=seg, scalar1=sm[:, 8 + tc:9 + tc], scalar2=sm[:, 12 + tc:13 + tc],
                        op0=ALU.mult, op1=ALU.add), reads=ks_ + ["sm2", "sm3"], writes=ks_)

        def uitem(c):
            if c % 8 == 0:
                (st["wu"],), st["wk"] = self.wacq()
            wu, wk = st["wu"], st["wk"]
            cl = c % 8
            g = c // 3
            i, i2 = c % 3, c % 2
            sq, ks = self.ybuf[:, i, :], f"yb{i}"
            ug, ku = self.ybuf[:, 3 + i2, :], f"yb{3 + i2}"
            sv, kv = self.ybuf[:, 5 + i, :], f"yb{5 + i}"
            bu = self.bank()
            for kc in range(8):
                self.mm(bu, (0, TT), wu[:, kc, cl * 128:(cl + 1) * 128], self.hT[:, kc, :], kc == 0, kc == 7, [wk, f"h{kc}"], kc == 7)
            pz = self.ps[:, bu, :]
            yield
            bs_ = self.bank()
            nt_ = (c * 128) // 512
            for tc in range(4):
                self.mm(bs_, (tc * 128, (tc + 1) * 128), vbf[:, tc, c * 128:(c + 1) * 128], self.wsTm[:, j, g, :], True, True,
                        [f"ar_v{tc}_{nt_}", "wsTm"], tc == 3)
            self.gelu_a(pz, [f"ps{bu}"], sq, ks)
            S_.op("dve", lambda: nc.vector.scalar_tensor_tensor(
                out=sv.rearrange("p (a t) -> p a t", a=4), in0=self.ps[:, bs_, :].rearrange("p (a t) -> p a t", a=4),
                scalar=self.P(f"lg{j}", c), in1=Ct[:, c, :].unsqueeze(1).broadcast_to([128, 4, 128]),
                op0=ALU.mult, op1=ALU.add), reads=[f"ps{bs_}", "prm", "ar_Ct"], writes=[kv])
            yield
            self.gelu_b(sq, ks)
            S_.op("dve", lambda: nc.vector.scalar_tensor_tensor(out=ug, in0=sq, scalar=1.0, in1=pz, op0=ALU.add, op1=ALU.mult),
                  reads=[ks, f"ps{bu}"], writes=[ku])
            yield
            S_.op("pool", lambda: nc.gpsimd.tensor_tensor(out=self.gated[:, c, :], in0=ug, in1=sv, op=ALU.mult),
                  reads=[ku, kv], writes=[f"g{c}"])

        self.pipeline(uitem(c) for c in range(24))
        self.downproj_post(24, lambda kc: (self.gated[:, kc, :], f"g{kc}"), 4, 0.5, None, f"nmo{L}")

    def attention(self, L, j, t):
        nc, S_ = self.nc, self.S_
        S_.fence("ar_")
        a = self.carve([("qT", 8 * TT, BF16), ("kTA", 4 * TT, BF16), ("kTB", 4 * TT, BF16), ("V", 4 * 512, BF16),
                        ("aT", 8 * TT, BF16), ("es16", 16, F32),
                        ("PX0", TT, BF16), ("PY0", TT, BF16), ("PX1", TT, BF16), ("PY1", TT, BF16),
                        ("lt", TT, F32), ("rl", TT, F32)])
        qT = a["qT"].rearrange("p (c t) -> p c t", c=8)
        kTA = a["kTA"].rearrange("p (g t) -> p g t", g=4)
        kTB = a["kTB"].rearrange("p (g t) -> p g t", g=4)
        S_.op("pool", lambda: nc.gpsimd.memset(a["kTA"][64:128, :], 0.0), writes=["ar_kz"])
        S_.op("pool", lambda: nc.gpsimd.memset(a["kTB"][0:64, :], 0.0), writes=["ar_kz"])
        V = a["V"].rearrange("p (b g u d) -> p b g u d", b=4, g=4, u=2)
        aT = a["aT"].rearrange("p (c t) -> p c t", c=8)
        self.prenorm(f"nmp{L}")
        S_.op("act", lambda: nc.scalar.activation(out=a["es16"], in_=self.P(f"sk{j}", 0, 16), func=AF.Exp),
              reads=["prm"], writes=["ar_es16"])
        st = {}

        def qitem(c):
            if c == 0:
                (st["wq"],), st["wkq"] = self.wacq()
            b = self.bank()
            for kc in range(8):
                self.mm(b, (0, TT), st["wq"][:, kc, c * 128:(c + 1) * 128], self.hT[:, kc, :], kc == 0, kc == 7, [st["wkq"], f"h{kc}"], kc == 7)
            yield
            S_.op("act", lambda: nc.scalar.activation(out=qT[:, c, :], in_=self.ps[:, b, :], func=AF.Identity,
                                                      bias=self.P(f"bq{j}", c), scale=1.0),
                  reads=[f"ps{b}", "prm"], writes=[f"ar_q{c}"])

        def kitem(g):
            if g == 0:
                (st["wkd"], st["wvv"]), st["wkk"] = self.wacq()
            b = self.bank()
            for kc in range(8):
                self.mm(b, (0, TT), st["wkd"][:, kc, g * 128:(g + 1) * 128], self.hT[:, kc, :], kc == 0, kc == 7, [st["wkk"], f"h{kc}"], kc == 7)
            yield
            S_.op("act", lambda: nc.scalar.activation(out=kTA[0:64, g, :], in_=self.ps[0:64, b, :], func=AF.Identity,
                                                      bias=self.P(f"bk{j}", g)[0:64, :], scale=1.0),
                  reads=[f"ps{b}", "prm"], writes=[f"ar_k{g}"])
            S_.op("act", lambda: nc.scalar.activation(out=kTB[64:128, g, :], in_=self.ps[64:128, b, :], func=AF.Identity,
                                                      bias=self.P(f"bk{j}", g)[64:128, :], scale=1.0),
                  reads=[f"ps{b}", "prm"], writes=[f"ar_k{g}"])

        bv3 = self.P(f"bv{j}", 0, 256).rearrange("p (g d) -> p g d", g=4)

        def vitem(tb):
            b = self.bank()
            for kc in range(8):
                self.mm(b, (0, 256), self.hT[:, kc, tb * 128:(tb + 1) * 128], st["wvv"][:, kc, :], kc == 0, kc == 7, [st["wkk"], f"h{kc}"], kc == 7)
            yield
            for u in range(2):
                S_.op("dve", lambda: nc.vector.tensor_tensor(
                    out=V[:, tb, :, u, :], in0=self.ps[:, b, 0:256].rearrange("p (g d) -> p g d", g=4), in1=bv3, op=ALU.add),
                    reads=[f"ps{b}", "prm"], writes=[f"ar_V{tb}"])

        self.pipeline([qitem(c) for c in range(8)] + [kitem(g) for g in range(4)] + [vitem(tb) for tb in range(4)])
        kcA, kcB = self.kcarry[:, j, 0], self.kcarry[:, j, 1]
        vc_ap = self.vcarry[:, j].rearrange("p (g u d) -> p g u d", g=4, u=2)
        ab = self.abias[:].rearrange("p h (xy g n) -> p h xy g n", xy=2, g=4)

        def aitem(n):
            qb, g = n // 4, n % 4
            has_prev = (t * 4 + qb) > 0
            qs = slice(qb * 128, (qb + 1) * 128)
            kbs = [0, 1] if has_prev else [1]
            c0 = kbs[0] * 256
            PXn, PYn = f"PX{n % 2}", f"PY{n % 2}"
            bX = self.bank()
            bY = self.bank()
            qk = [f"ar_q{2 * g}", f"ar_q{2 * g + 1}"]
            for i, kb in enumerate(kbs):
                if kb == 1:
                    kl0, kl1, kkey = kTA[:, g, qs], kTB[:, g, qs], f"ar_k{g}"
                elif qb > 0:
                    ps_ = slice((qb - 1) * 128, qb * 128)
                    kl0, kl1, kkey = kTA[:, g, ps_], kTB[:, g, ps_], f"ar_k{g}"
                else:
                    kl0, kl1, kkey = kcA[:, g, :], kcB[:, g, :], "kcarry"
                self.mm(bX, (kb * 256, kb * 256 + 256), kl0, qT[:, 2 * g:2 * g + 2, qs], i == 0, False, [kkey, "ar_kz"] + qk, False)
                self.mm(bY, (kb * 256, kb * 256 + 256), kl1, qT[:, 2 * g:2 * g + 2, qs], i == 0, False, [kkey, "ar_kz"] + qk, False)
            for (bb, xy) in ((bX, 0), (bY, 1)):
                for hl in range(2):
                    self.mm(bb, (c0, TT), self.identB[:], ab[:, hl, xy, g, c0:TT], False, hl == 1, ["identB", "abias"], hl == 1)
            yield
            for (bb, Pn) in ((bX, PXn), (bY, PYn)):
                S_.op("act", lambda: nc.scalar.activation(out=a[Pn][:, c0:TT], in_=self.ps[:, bb, c0:TT],
                                                          func=AF.Exp, scale=0.125),
                      reads=[f"ps{bb}"], writes=["ar_" + Pn])
            yield
            bo = self.bank()
            bl = self.bank()
            for (bank_, is_o) in ((bo, True), (bl, False)):
                for (xy, Pn) in ((0, PXn), (1, PYn)):
                    for ki, kb in enumerate(kbs):
                        if not is_o:
                            lhs, lk = self.ones1[:], "ones1"
                        elif kb == 1:
                            lhs, lk = V[:, qb, g].rearrange("p u d -> p (u d)"), f"ar_V{qb}"
                        elif qb > 0:
                            lhs, lk = V[:, qb - 1, g].rearrange("p u d -> p (u d)"), f"ar_V{qb - 1}"
                        else:
                            lhs, lk = vc_ap[:, g].rearrange("p u d -> p (u d)"), "vcarry"
                        self.mm(bank_, (xy * 256, xy * 256 + 256), lhs, a[Pn][:, kb * 256:kb * 256 + 256],
                                ki == 0, ki == len(kbs) - 1, [lk, "ar_" + Pn], ki == len(kbs) - 1)
            yield
            ltn = a["lt"][:, (n % 2) * 256:(n % 2) * 256 + 256]
            rln = a["rl"][:, (n % 2) * 256:(n % 2) * 256 + 256]
            kl_, kr_ = f"ar_lt{n % 2}", f"ar_rl{n % 2}"
            for hf in range(2):
                pr = slice(hf * 64, hf * 64 + 64)
                for hh in range(2):
                    S_.op("dve", lambda: nc.vector.tensor_scalar(
                        out=ltn[pr, hh * 128:(hh + 1) * 128],
                        in0=self.ps[pr, bl, hf * 256 + hh * 128:hf * 256 + (hh + 1) * 128],
                        scalar1=a["es16"][pr, 4 * g + 2 * hf + hh:4 * g + 2 * hf + hh + 1], scalar2=None, op0=ALU.add),
                        reads=[f"ps{bl}", "ar_es16"], writes=[f"{kl_}_{hf}{hh}"])
            S_.op("act", lambda: nc.scalar.activation(out=ltn, in_=ltn, func=AF.Ln),
                  reads=[f"{kl_}_{hf}{hh}" for hf in range(2) for hh in range(2)],
                  writes=[f"{kl_}_{hf}{hh}" for hf in range(2) for hh in range(2)])
            S_.op("act", lambda: nc.scalar.activation(out=rln, in_=ltn, func=AF.Exp, scale=-1.0),
                  reads=[f"{kl_}_{hf}{hh}" for hf in range(2) for hh in range(2)], writes=[kr_])
            yield
            for hf in range(2):
                pr = slice(hf * 64, hf * 64 + 64)
                cs = slice(hf * 256, hf * 256 + 256)
                S_.op("dve", lambda: nc.vector.tensor_tensor(
                    out=aT[pr, 2 * g:2 * g + 2, qs], in0=self.ps[pr, bo, cs].rearrange("p (a q) -> p a q", a=2),
                    in1=rln[pr, :].rearrange("p (a q) -> p a q", a=2), op=ALU.mult),
                    reads=[f"ps{bo}", kr_], writes=[f"ar_aT{2 * g}_{hf}", f"ar_aT{2 * g + 1}_{hf}"])

        self.nbank = 8
        self.pipeline(aitem(n) for n in range(16))
        self.nbank = 7
        self.pb = self.pb % 7
        S_.op("pool", lambda: nc.gpsimd.tensor_copy(out=kcA[0:64], in_=kTA[0:64, :, 3 * 128:4 * 128]),
              reads=[f"ar_k{g}" for g in range(4)], writes=["kcarry"])
        S_.op("pool", lambda: nc.gpsimd.tensor_copy(out=kcB[64:128], in_=kTB[64:128, :, 3 * 128:4 * 128]),
              reads=[f"ar_k{g}" for g in range(4)], writes=["kcarry"])
        S_.op("pool", lambda: nc.gpsimd.tensor_copy(out=self.vcarry[:, j], in_=V[:, 3].rearrange("p g u d -> p (g u d)")),
              reads=["ar_V3"], writes=["vcarry"])
        self.downproj_post(8, lambda kc: (aT[:, kc, :], [f"ar_aT{kc}_0", f"ar_aT{kc}_1"]), 1, 1.0, f"bo{j}", f"nmo{L}")


_CACHE = {}


def _get_nc(S, off, nprm, nlayers=4):
    key = (S, nprm, nlayers)
    if key not in _CACHE:
        _CACHE[key] = Builder(S, off, nprm, nlayers).build()
    return _CACHE[key]


def run(inputs, xs, nlayers=4):
    prm, off, wsT, maskT, bias = pack_params(inputs)
    S = xs[0].shape[0]
    nc = _get_nc(S, off, prm.shape[1], nlayers)
    f = lambda k: np.ascontiguousarray(np.asarray(inputs[k], np.float32))
    shared = {"prm": prm, "wsT": wsT, "maskT": maskT, "abias": bias,
              "attn_w_qkv": f("attn_w_qkv"), "attn_w_o": f("attn_w_o"),
              "sgu_w_in": f("sgu_w_in"), "sgu_w_out": f("sgu_w_out"),
              "ffn_w_in": f("ffn_w_in"), "ffn_w_out": f("ffn_w_out")}
    in_maps = [dict(shared, x=np.ascontiguousarray(np.asarray(xc, np.float32))) for xc in xs]
    res = run_bass_kernel_spmd(nc, in_maps, core_ids=list(range(len(xs))))
    return [np.asarray(r["y"], np.float32) for r in res.results]


def kernel(**inputs):
    x = np.asarray(inputs["x"], np.float32)
    ys = run(inputs, [x[b] for b in range(x.shape[0])])
    return np.stack(ys, axis=0).astype(np.float32)
```
